# Optimizing a Trainium2 kernel written in Bass

```python
import jax, jax.numpy as jnp
from jax import lax
import numpy as np

D_MODEL = 4096
BATCH = 4
SEQ = 4096
DEPTH = 1

HEAD_DIM = 128
MLA_HEADS = 16
MLA_Q_RANK = 1024
MLA_KV_RANK = 512
MLA_NOPE = 128
MLA_ROPE = 64
MLA_V = 128
DIL_HEADS = 8
DIL_PATTERNS = ((128, 1), (512, 4), (2048, 16))
DIL_GROUPS = len(DIL_PATTERNS)
N_BRANCHES = 2
D_FF = -(-8 * D_MODEL // (3 * 256)) * 256
ROPE_THETA = 10000.0
NORM_EPS = 1e-6
Q_BLOCK = 128
N_MOD = 6

IN_Q = MLA_Q_RANK
IN_KV = MLA_KV_RANK
IN_KR = MLA_ROPE
IN_DIL = 3 * DIL_GROUPS * DIL_HEADS * HEAD_DIM
IN_GATE = N_BRANCHES * D_MODEL
IN_TOTAL = IN_Q + IN_KV + IN_KR + IN_DIL + IN_GATE
IN_SPLITS = (IN_Q, IN_Q + IN_KV, IN_Q + IN_KV + IN_KR, IN_Q + IN_KV + IN_KR + IN_DIL)

kernel_name = 'hybrid_mla_dilated_adaln_block'


def rmsnorm(x, g):
    xf = x.astype(jnp.float32)
    y = xf * lax.rsqrt(jnp.mean(xf * xf, axis=-1, keepdims=True) + NORM_EPS)
    return (y * g.astype(jnp.float32)).astype(x.dtype)


def modulate(h, shift, scale):
    return h * (1 + scale[:, None, :]) + shift[:, None, :]


def rope_tables(positions, dim):
    inv = ROPE_THETA ** (-jnp.arange(0, dim, 2, dtype=jnp.float32) / dim)
    ang = positions.astype(jnp.float32)[..., None] * inv
    return jnp.cos(ang), jnp.sin(ang)


def apply_rope(x, cos, sin):
    half = x.shape[-1] // 2
    shp = cos.shape[:2] + (1,) * (x.ndim - 3) + (half,)
    cos = cos.reshape(shp).astype(x.dtype)
    sin = sin.reshape(shp).astype(x.dtype)
    x1, x2 = x[..., :half], x[..., half:]
    return jnp.concatenate([x1 * cos - x2 * sin, x2 * cos + x1 * sin], axis=-1)


def mla_attention(q_nope, q_rope, k_nope, k_rope, v):
    B, S, H, _ = q_nope.shape
    scale = (MLA_NOPE + MLA_ROPE) ** -0.5

    def block(i):
        start = i * Q_BLOCK
        qn = lax.dynamic_slice_in_dim(q_nope, start, Q_BLOCK, axis=1)
        qr = lax.dynamic_slice_in_dim(q_rope, start, Q_BLOCK, axis=1)
        s = (jnp.einsum('bqhd,bkhd->bhqk', qn, k_nope)
             + jnp.einsum('bqhr,bkr->bhqk', qr, k_rope)).astype(jnp.float32) * scale
        p = jax.nn.softmax(s, axis=-1)
        return jnp.einsum('bhqk,bkhd->bqhd', p.astype(v.dtype), v)

    out = lax.map(block, jnp.arange(S // Q_BLOCK))
    return out.transpose(1, 0, 2, 3, 4).reshape(B, S, H * MLA_V)


def dilated_attention(q, k, v):
    B, S, G, H, dh = q.shape
    scale = dh ** -0.5

    def block(i):
        start = i * Q_BLOCK
        qpos = start + jnp.arange(Q_BLOCK)
        qb = lax.dynamic_slice_in_dim(q, start, Q_BLOCK, axis=1)
        ms, ls, nums = [], [], []
        for g, (w, d) in enumerate(DIL_PATTERNS):
            n_side = w // (2 * d)
            offs = jnp.arange(-n_side, n_side + 1) * d
            idx = qpos[:, None] + offs[None, :]
            valid = (idx >= 0) & (idx < S)
            idx = jnp.clip(idx, 0, S - 1)
            kg = jnp.take(k[:, :, g], idx, axis=1)
            vg = jnp.take(v[:, :, g], idx, axis=1)
            s = jnp.einsum('bqhd,bqkhd->bhqk', qb[:, :, g], kg).astype(jnp.float32) * scale
            s = jnp.where(valid[None, None], s, -jnp.inf)
            m = jnp.max(s, axis=-1)
            p = jnp.exp(s - m[..., None])
            ms.append(m)
            ls.append(jnp.sum(p, axis=-1))
            nums.append(jnp.einsum('bhqk,bqkhd->bhqd', p, vg.astype(jnp.float32)))
        m_all = jnp.stack(ms)
        wts = jnp.exp(m_all - jnp.max(m_all, axis=0))
        den = jnp.sum(wts * jnp.stack(ls), axis=0)
        num = jnp.sum(wts[..., None] * jnp.stack(nums), axis=0)
        return (num / den[..., None]).astype(q.dtype)

    out = lax.map(block, jnp.arange(S // Q_BLOCK))
    return out.transpose(1, 0, 3, 2, 4).reshape(B, S, H * dh)


def setup_inputs(seed: int = 0) -> dict:
    key = jax.random.key(seed)
    ks = jax.random.split(key, 24)

    def nrm(k, shape, fan_in):
        return jax.random.normal(k, shape, jnp.float32) * (fan_in ** -0.5)

    def gain(k, shape):
        return 1.0 + 0.02 * jax.random.normal(k, shape, jnp.float32)

    L = DEPTH
    return {
        'x': jax.random.normal(ks[0], (BATCH, SEQ, D_MODEL), jnp.float32),
        'c': jax.random.normal(ks[1], (BATCH, D_MODEL), jnp.float32),
        'positions': jnp.broadcast_to(jnp.arange(SEQ, dtype=jnp.int32), (BATCH, SEQ)),
        'w_ada': nrm(ks[2], (L, D_MODEL, N_MOD * D_MODEL), D_MODEL),
        'b_ada': 0.02 * jax.random.normal(ks[3], (L, N_MOD * D_MODEL), jnp.float32),
        'norm1_g': gain(ks[4], (L, D_MODEL)),
        'w_in': nrm(ks[5], (L, D_MODEL, IN_TOTAL), D_MODEL),
        'q_norm_g': gain(ks[6], (L, MLA_Q_RANK)),
        'w_uq': nrm(ks[7], (L, MLA_Q_RANK, MLA_HEADS * (MLA_NOPE + MLA_ROPE)), MLA_Q_RANK),
        'kv_norm_g': gain(ks[8], (L, MLA_KV_RANK)),
        'w_ukv': nrm(ks[9], (L, MLA_KV_RANK, MLA_HEADS * (MLA_NOPE + MLA_V)), MLA_KV_RANK),
        'w_proj_a': nrm(ks[10], (L, MLA_HEADS * MLA_V, D_MODEL), MLA_HEADS * MLA_V),
        'w_proj_b': nrm(ks[11], (L, DIL_HEADS * HEAD_DIM, D_MODEL), DIL_HEADS * HEAD_DIM),
        'w_out': nrm(ks[12], (L, D_MODEL, D_MODEL), D_MODEL),
        'norm2_g': gain(ks[13], (L, D_MODEL)),
        'w_gate': nrm(ks[14], (L, D_MODEL, D_FF), D_MODEL),
        'w_up': nrm(ks[15], (L, D_MODEL, D_FF), D_MODEL),
        'w_down': nrm(ks[16], (L, D_FF, D_MODEL), D_FF),
        'final_g': gain(ks[17], (D_MODEL,)),
    }


def reference(x, c, positions, w_ada, b_ada, norm1_g, w_in, q_norm_g, w_uq, kv_norm_g,
              w_ukv, w_proj_a, w_proj_b, w_out, norm2_g, w_gate, w_up, w_down, final_g):
    B, S, D = x.shape
    cos_f, sin_f = rope_tables(positions, HEAD_DIM)
    cos_r, sin_r = rope_tables(positions, MLA_ROPE)
    c_act = jax.nn.silu(c)

    for l in range(DEPTH):
        mod = c_act @ w_ada[l] + b_ada[l]
        sh1, sc1, g1, sh2, sc2, g2 = jnp.split(mod, N_MOD, axis=-1)

        h = modulate(rmsnorm(x, norm1_g[l]), sh1, sc1)
        z = h @ w_in[l]
        zq, zkv, zkr, zdil, zg = jnp.split(z, IN_SPLITS, axis=-1)

        q = (rmsnorm(zq, q_norm_g[l]) @ w_uq[l]).reshape(B, S, MLA_HEADS, MLA_NOPE + MLA_ROPE)
        q_nope = q[..., :MLA_NOPE]
        q_rope = apply_rope(q[..., MLA_NOPE:], cos_r, sin_r)
        kv = (rmsnorm(zkv, kv_norm_g[l]) @ w_ukv[l]).reshape(B, S, MLA_HEADS, MLA_NOPE + MLA_V)
        k_nope, v_a = kv[..., :MLA_NOPE], kv[..., MLA_NOPE:]
        k_rope = apply_rope(zkr[:, :, None, :], cos_r, sin_r)[:, :, 0, :]
        y_a = mla_attention(q_nope, q_rope, k_nope, k_rope, v_a)

        zd = zdil.reshape(B, S, 3, DIL_GROUPS, DIL_HEADS, HEAD_DIM)
        q_d = apply_rope(zd[:, :, 0], cos_f, sin_f)
        k_d = apply_rope(zd[:, :, 1], cos_f, sin_f)
        y_b = dilated_attention(q_d, k_d, zd[:, :, 2])

        gate_a, gate_b = jnp.split(jax.nn.sigmoid(zg), N_BRANCHES, axis=-1)
        mixed = gate_a * (y_a @ w_proj_a[l]) + gate_b * (y_b @ w_proj_b[l])
        x = x + g1[:, None, :] * (mixed @ w_out[l])

        h2 = modulate(rmsnorm(x, norm2_g[l]), sh2, sc2)
        ff = (jax.nn.silu(h2 @ w_gate[l]) * (h2 @ w_up[l])) @ w_down[l]
        x = x + g2[:, None, :] * ff

    return rmsnorm(x, final_g)
```

```python
import math
from contextlib import ExitStack

import numpy as np
import ml_dtypes

import concourse.bass as bass
import concourse.mybir as mybir
from concourse.bass_utils import run_bass_kernel_spmd

F32 = mybir.dt.float32
BF16 = mybir.dt.bfloat16
I32 = mybir.dt.int32
AF = mybir.ActivationFunctionType
ALU = mybir.AluOpType

SEQ = 4096
NOWN = 2048
NHALO = 1024
TS = 512
MLA_H = 16
DIL_H = 8
DIL_G = 3
QRANK = 1024
KVRANK = 512
EPS = 1e-6
THETA = 10000.0
PI = math.pi
NDS = 8
ENG = ("pe", "act", "dve", "pool", "sp")

DIL_SEQ = [(0, r) for r in (-1, 0, 1)] + [(1, r) for r in range(-2, 3)] + [(2, r) for r in range(-8, 9)]
DIL_D = (1, 4, 16)
DIL_HALF = (64, 256, 1024)
NSEQ = len(DIL_SEQ)


class Buf:
    __slots__ = ("t", "w", "r", "x")

    def __init__(self, t, x=False):
        self.t = t
        self.w = None
        self.r = {}
        self.x = x


class Prog:
    def __init__(self, nc, es):
        self.nc = nc
        self.es = es
        self.q = {e: [] for e in ENG}
        self.prog = {e: es.enter_context(nc.semaphore("pg_" + e)) for e in ENG}
        self.cnt = {e: 0 for e in ENG}
        self.waited = {}
        self.dsem = {q: [es.enter_context(nc.semaphore("d%s%d" % (q, i))) for i in range(NDS)]
                     for q in ("sp", "pool")}
        self.dcnt = {q: [0] * NDS for q in ("sp", "pool")}
        self.dnext = {q: 0 for q in ("sp", "pool")}
        self.uid = 0
        self.dead = False

    def _wait(self, eng, tok):
        if tok is None or self.dead:
            return
        sem, val, src, key = tok
        if src == eng and eng == "pe":
            return
        wk = (eng, key)
        if self.waited.get(wk, 0) >= val:
            return
        self.waited[wk] = val
        self.q[eng].append(lambda e: e.wait_ge(sem, val))

    def _deps(self, eng, reads, writes):
        for b in reads:
            self._wait(eng, b.w)
            if b.x:
                for t in b.r.values():
                    self._wait(eng, t)
        for b in writes:
            self._wait(eng, b.w)
            for t in b.r.values():
                self._wait(eng, t)

    def _mark(self, tok, reads, writes):
        for b in reads:
            b.r[tok[3]] = tok
        for b in writes:
            b.w = tok
            b.r = {}

    def op(self, eng, fn, reads=(), writes=()):
        if self.dead:
            return None
        self._deps(eng, reads, writes)
        self.cnt[eng] += 1
        sem = self.prog[eng]
        tok = (sem, self.cnt[eng], eng, "pg_" + eng)
        self.q[eng].append(lambda e: fn(e).then_inc(sem, 1))
        self._mark(tok, reads, writes)
        return tok

    def dma(self, q, out, in_, reads=(), writes=()):
        if self.dead:
            return None
        i = self.dnext[q]
        self.dnext[q] = (i + 1) % NDS
        sem = self.dsem[q][i]
        key = "d%s%d" % (q, i)
        if self.dcnt[q][i] > 0:
            self._wait(q, (sem, self.dcnt[q][i], "dma", key))
        self._deps(q, reads, writes)
        self.dcnt[q][i] += 16
        tok = (sem, self.dcnt[q][i], "dma", key)
        self.q[q].append(lambda e: e.dma_start(out=out, in_=in_).then_inc(sem, 16))
        self._mark(tok, reads, writes)
        return tok

    def barrier(self):
        if self.dead:
            return
        toks = [(self.prog[e], self.cnt[e], e, "pg_" + e) for e in ENG if self.cnt[e] > 0]
        for q in ("sp", "pool"):
            for i in range(NDS):
                if self.dcnt[q][i] > 0:
                    toks.append((self.dsem[q][i], self.dcnt[q][i], "dma", "d%s%d" % (q, i)))
        for e in ENG:
            for t in toks:
                if t[2] == e:
                    continue
                self._wait(e, t)

    def sb(self, es, shape, dt, name=None):
        self.uid += 1
        return Buf(es.enter_context(self.nc.sbuf_tensor("%s_%d" % (name or "sb", self.uid), list(shape), dt)))

    def ps(self, es, shape, dt, name=None):
        self.uid += 1
        return Buf(es.enter_context(self.nc.psum_tensor("%s_%d" % (name or "ps", self.uid), list(shape), dt)), True)


class Ring:
    def __init__(self, bufs):
        self.bufs = bufs
        self.i = 0

    def next(self):
        b = self.bufs[self.i]
        self.i = (self.i + 1) % len(self.bufs)
        return b


class WStream:
    SLOT = 8192

    def __init__(self, P, es, n=4):
        self.P = P
        self.ring = Ring([P.sb(es, [128, self.SLOT], BF16, "w") for _ in range(n)])

    def get(self, wap, kb, cw, kp=128):
        b = self.ring.next()
        view = b.t[0:kp, 0:kb * cw].rearrange("p (k c) -> p k c", k=kb)
        self.P.dma("pool", view, wap.rearrange("(k p) c -> p k c", p=kp), writes=[b])
        return b, view


def kgroups(n, g=16):
    out = []
    k = 0
    while k < n:
        out.append((k, min(g, n - k)))
        k += g
    return out


class _Stop(Exception):
    pass


def build_program(D, DFF, stop=0):
    KC = D // 128
    NFF = DFF // 128
    NMC = D // 128
    nc = bass.Bass("TRN2", target_bir_lowering=False)

    def din(name, shape, dt=F32):
        return nc.dram_tensor(name, list(shape), dt, kind="ExternalInput").ap()

    import os
    _dbg = bool(os.environ.get("KDBG"))

    def dscr(name, shape, dt=BF16):
        return nc.dram_tensor(name, list(shape), dt, kind="ExternalOutput" if _dbg else "Internal").ap()

    x_d = din("x", [SEQ, D])
    pos_d = din("pos", [1, SEQ], I32)
    c_d = din("c_t", [128, KC])
    wada_d = din("w_ada", [D, 6 * D])
    bada_d = din("b_ada", [1, 6 * D])
    n1g_d = din("n1g", [128, KC])
    n2g_d = din("n2g", [128, KC])
    qg_d = din("qg", [128, QRANK // 128])
    kvg_d = din("kvg", [128, KVRANK // 128])
    fg_d = din("fg", [1, D])
    win_d = din("w_in", [D, 1600 + 9216 + 2 * D])
    wkrp_d = din("w_krp", [D, 64])
    wuq_d = din("w_uq", [QRANK, MLA_H * 256])
    wukv_d = din("w_ukv", [KVRANK, MLA_H * 256])
    wpa_d = din("w_pa", [MLA_H * 128, D])
    wpb_d = din("w_pb", [DIL_H * 128, D])
    wout_d = din("w_out", [D, D])
    wg_d = din("w_gate", [D, DFF])
    wu_d = din("w_up", [D, DFF])
    wd_d = din("w_down", [DFF, D])
    ident_d = din("ident", [128, 128], BF16)
    masks_d = din("masks", [128, NSEQ * 128], BF16)
    rope_d = din("ropec", [128, 4])
    out_d = nc.dram_tensor("out", [NOWN, D], F32, kind="ExternalOutput").ap()

    modrow = dscr("modrow", [1, 6 * D], F32)
    KR = dscr("KR", [64, SEQ])
    QN = dscr("QN", [MLA_H, 128, NOWN])
    QR = dscr("QR", [MLA_H, 64, NOWN])
    KN = dscr("KN", [MLA_H, 128, SEQ])
    VA = dscr("VA", [MLA_H, 128, SEQ // 128, 129])
    NKD = NOWN + NHALO
    QD = dscr("QD", [24, 128, NOWN])
    KD = dscr("KD", [24, 128, NKD])
    VD = dscr("VD", [24, 128, NKD // 128, 129])
    GT = dscr("GT", [2 * NMC, 128, NOWN])
    YT = dscr("YT", [24, 128, NOWN])
    X1 = dscr("X1", [NOWN, D], F32)

    C_Q, C_KV, C_KR, C_DQ = 0, 1024, 1536, 1600
    C_DK, C_DV, C_G = 1600 + 3072, 1600 + 6144, 1600 + 9216

    with ExitStack() as es:
        P = Prog(nc, es)
        ident = P.sb(es, [128, 128], BF16, "ident")
        ones_bf = P.sb(es, [128, 128], BF16, "ones")
        ones_f = P.sb(es, [1, 128], F32, "onesf")
        modT = P.sb(es, [128, 6 * KC], F32, "modT")
        G1 = P.sb(es, [128, KC], F32, "G1")
        G2 = P.sb(es, [128, KC], F32, "G2")
        n1g = P.sb(es, [128, KC], F32, "n1g")
        n2g = P.sb(es, [128, KC], F32, "n2g")
        qg = P.sb(es, [128, 8], F32, "qg")
        kvg = P.sb(es, [128, 4], F32, "kvg")
        ropec = P.sb(es, [128, 4], F32, "ropec")
        consts = [ident, ones_bf, ones_f, modT, G1, G2, n1g, n2g, qg, kvg, ropec]

        P.dma("sp", ident.t[:], ident_d, writes=[ident])
        P.dma("sp", n1g.t[:], n1g_d, writes=[n1g])
        P.dma("sp", n2g.t[:], n2g_d, writes=[n2g])
        P.dma("sp", qg.t[:], qg_d, writes=[qg])
        P.dma("sp", kvg.t[:], kvg_d, writes=[kvg])
        P.dma("sp", ropec.t[:], rope_d, writes=[ropec])
        P.op("dve", lambda e: e.memset(ones_bf.t[:], 1.0), writes=[ones_bf])
        P.op("dve", lambda e: e.memset(ones_f.t[:], 1.0), writes=[ones_f])

        if stop == -1:
            P.barrier()
            P.dead = True

        def chk(n):
            if stop == n:
                P.barrier()
                P.dead = True

        try:
            if stop == -2:
                P.barrier()
                raise _Stop()
            with ExitStack() as ph:
                W = WStream(P, ph)
                cf = P.sb(ph, [128, KC], F32, "cf")
                cact = P.sb(ph, [128, KC], BF16, "cact")
                rowr = Ring([P.sb(ph, [1, 512], F32, "row") for _ in range(2)])
                brr = Ring([P.sb(ph, [1, 512], F32, "brow") for _ in range(2)])
                pmr = Ring([P.ps(ph, [128, 512], F32, "pm") for _ in range(2)])
                ptr = Ring([P.ps(ph, [128, 512], F32, "pt") for _ in range(2)])
                P.dma("sp", cf.t[:], c_d, writes=[cf])
                P.op("act", lambda e: e.activation(out=cact.t[:], in_=cf.t[:], func=AF.Silu),
                     reads=[cf], writes=[cact])
                for cb in range(6 * D // 512):
                    pm = pmr.next()
                    for (k0, kb) in kgroups(KC):
                        wb, wv = W.get(wada_d[k0 * 128:(k0 + kb) * 128, cb * 512:(cb + 1) * 512], kb, 512)

                        def f(e, k0=k0, kb=kb, wv=wv, pm=pm):
                            for k in range(kb):
                                ins = e.matmul(pm.t[0:1, :], lhsT=cact.t[:, k0 + k:k0 + k + 1], rhs=wv[:, k, :],
                                               start=(k0 + k == 0), stop=(k0 + k == KC - 1))
                            return ins
                        P.op("pe", f, reads=[wb, cact], writes=[pm])
                    br = brr.next()
                    row = rowr.next()
                    P.dma("sp", br.t[:], bada_d[0:1, cb * 512:(cb + 1) * 512], writes=[br])
                    P.op("dve", lambda e, row=row, pm=pm, br=br: e.tensor_tensor(
                        out=row.t[:], in0=pm.t[0:1, :], in1=br.t[:], op=ALU.add), reads=[pm, br], writes=[row])
                    P.dma("sp", modrow[0:1, cb * 512:(cb + 1) * 512], row.t[:], reads=[row])
                    pt = ptr.next()

                    def f2(e, row=row, pt=pt):
                        for j in range(4):
                            ins = e.matmul(pt.t[:, j:j + 1], lhsT=row.t[0:1, j * 128:(j + 1) * 128],
                                           rhs=ones_f.t[0:1, 0:1], start=True, stop=True)
                        return ins
                    P.op("pe", f2, reads=[row, ones_f], writes=[pt])
                    P.op("dve", lambda e, pt=pt, cb=cb: e.tensor_copy(out=modT.t[:, cb * 4:cb * 4 + 4], in_=pt.t[:, 0:4]),
                         reads=[pt], writes=[modT])
                P.op("dve", lambda e: e.scalar_tensor_tensor(out=G1.t[:], in0=modT.t[:, KC:2 * KC], scalar=1.0,
                                                             in1=n1g.t[:], op0=ALU.add, op1=ALU.mult),
                     reads=[modT, n1g], writes=[G1])
                P.op("dve", lambda e: e.scalar_tensor_tensor(out=G2.t[:], in0=modT.t[:, 4 * KC:5 * KC], scalar=1.0,
                                                             in1=n2g.t[:], op0=ALU.add, op1=ALU.mult),
                     reads=[modT, n2g], writes=[G2])
                P.barrier()
                chk(1)

            def rsqrt_ops(src_ap, src_bufs, dst_ap, dst_buf, scale):
                P.op("act", lambda e: e.activation(out=dst_ap, in_=src_ap, func=AF.Sqrt, scale=scale, bias=EPS),
                     reads=src_bufs, writes=[dst_buf])
                P.op("dve", lambda e: e.reciprocal(out=dst_ap, in_=dst_ap), reads=[dst_buf], writes=[dst_buf])

            def rope_table(posf_ap, posf_buf, inv_ap, phase, scale, dst_ap, dst_buf, tA, tA_ap, tI, tI_ap, tB, tB_ap):
                P.op("dve", lambda e: e.tensor_scalar(out=tA_ap, in0=posf_ap, scalar1=inv_ap, scalar2=phase,
                                                      op0=ALU.mult, op1=ALU.add), reads=[posf_buf, ropec], writes=[tA])
                P.op("dve", lambda e: e.tensor_copy(out=tI_ap, in_=tA_ap), reads=[tA], writes=[tI])
                P.op("dve", lambda e: e.tensor_copy(out=tB_ap, in_=tI_ap), reads=[tI], writes=[tB])
                P.op("dve", lambda e: e.tensor_tensor(out=tB_ap, in0=tA_ap, in1=tB_ap, op=ALU.subtract),
                     reads=[tA, tB], writes=[tB])
                P.op("dve", lambda e: e.scalar_tensor_tensor(out=tA_ap, in0=tB_ap, scalar=0.5, in1=tB_ap,
                                                             op0=ALU.is_gt, op1=ALU.subtract),
                     reads=[tB], writes=[tA])
                P.op("act", lambda e: e.activation(out=dst_ap, in_=tA_ap, func=AF.Sin, scale=scale),
                     reads=[tA, ropec], writes=[dst_buf])

            def build_hT(ph, src_rows, Gt, sh_col0, hT):
                with ExitStack() as sc:
                    xr = Ring([P.sb(sc, [128, D], F32, "xt") for _ in range(2)])
                    xsr = Ring([P.sb(sc, [128, D], BF16, "xs") for _ in range(2)])
                    ssr = Ring([P.sb(sc, [128, 2], F32, "ss") for _ in range(2)])
                    ptr = Ring([P.ps(sc, [128, 8, 128], BF16, "ptT") for _ in range(3)])
                    for tt in range(TS // 128):
                        xt = xr.next()
                        xs = xsr.next()
                        ss = ssr.next()
                        P.dma("sp", xt.t[:], src_rows(tt), writes=[xt])
                        P.op("dve", lambda e, ss=ss: e.memset(ss.t[:], 0.0), writes=[ss])
                        P.op("act", lambda e, xt=xt, xs=xs, ss=ss: e.activation(
                            out=xs.t[:], in_=xt.t[:], func=AF.Square, accum_out=ss.t[:, 0:1]),
                            reads=[xt], writes=[xs, ss])
                        rsqrt_ops(ss.t[:, 0:1], [ss], ss.t[:, 1:2], ss, 1.0 / D)
                        P.op("act", lambda e, xt=xt, xs=xs, ss=ss: e.activation(
                            out=xs.t[:], in_=xt.t[:], func=AF.Identity, scale=ss.t[:, 1:2]),
                            reads=[xt, ss], writes=[xs])
                        for k0 in range(0, KC, 8):
                            kn = min(8, KC - k0)
                            pt = ptr.next()

                            def f(e, k0=k0, kn=kn, pt=pt, xs=xs):
                                for j in range(kn):
                                    ins = e.transpose(pt.t[:, j, :], xs.t[:, (k0 + j) * 128:(k0 + j + 1) * 128], ident.t[:])
                                return ins
                            P.op("pe", f, reads=[xs, ident], writes=[pt])
                            for j in range(kn):
                                kc = k0 + j
                                if kc % 2 == 0:
                                    P.op("act", lambda e, pt=pt, j=j, kc=kc, tt=tt: e.activation(
                                        out=hT.t[:, kc, tt * 128:(tt + 1) * 128], in_=pt.t[:, j, :], func=AF.Identity,
                                        scale=Gt.t[:, kc:kc + 1], bias=modT.t[:, sh_col0 + kc:sh_col0 + kc + 1]),
                                        reads=[pt, Gt, modT], writes=[hT])
                                else:
                                    P.op("dve", lambda e, pt=pt, j=j, kc=kc, tt=tt: e.tensor_scalar(
                                        out=hT.t[:, kc, tt * 128:(tt + 1) * 128], in0=pt.t[:, j, :],
                                        scalar1=Gt.t[:, kc:kc + 1], scalar2=modT.t[:, sh_col0 + kc:sh_col0 + kc + 1],
                                        op0=ALU.mult, op1=ALU.add), reads=[pt, Gt, modT], writes=[hT])
                    P.barrier()

            def gemm_fm(W, wap, kch, act_of, ncols, M, psr, epi, act_bufs, kp=128):
                cbmax = max(M, min(512, (WStream.SLOT // kch) // M * M))
                for c0 in range(0, ncols, cbmax):
                    cw = min(cbmax, ncols - c0)
                    wb, wv = W.get(wap[:, c0:c0 + cw], kch, cw, kp)
                    for m0 in range(0, cw, M):
                        ps = psr.next()

                        def f(e, wv=wv, m0=m0, ps=ps):
                            for k in range(kch):
                                ins = e.matmul(ps.t[0:M, :], lhsT=wv[:, k, m0:m0 + M], rhs=act_of(k),
                                               start=(k == 0), stop=(k == kch - 1))
                            return ins
                        P.op("pe", f, reads=[wb] + act_bufs, writes=[ps])
                        epi((c0 + m0) // M, ps)

            def gemm_tm(W, wap, kch, actT_of, ncols, psets, epi, act_bufs):
                kg = kgroups(kch)
                for c0 in range(0, ncols, 512):
                    cw = min(512, ncols - c0)
                    pset = psets.next()
                    for (k0, kb) in kg:
                        wb, wv = W.get(wap[k0 * 128:(k0 + kb) * 128, c0:c0 + cw], kb, cw)

                        def f(e, wv=wv, k0=k0, kb=kb, pset=pset, cw=cw):
                            for tt in range(4):
                                for k in range(kb):
                                    ins = e.matmul(pset[tt].t[:, 0:cw], lhsT=actT_of(k0 + k, tt), rhs=wv[:, k, :],
                                                   start=(k0 + k == 0), stop=(k0 + k == kch - 1))
                            return ins
                        P.op("pe", f, reads=[wb] + act_bufs, writes=list(pset))
                    for tt in range(4):
                        epi(c0, cw, tt, pset[tt])

            slabs = [("own", s * TS) for s in range(NOWN // TS)] + \
                    [("halo", NOWN + s * TS) for s in range(NHALO // TS)] + \
                    [("far", NOWN + NHALO + s * TS) for s in range((SEQ - NOWN - NHALO) // TS)]
            for (kind, t0) in slabs:
                with ExitStack() as ph:
                    hT = P.sb(ph, [128, KC, TS], BF16, "hT")
                    build_hT(ph, lambda tt, t0=t0: x_d[t0 + tt * 128:t0 + (tt + 1) * 128, :], G1, 0, hT)
                    chk(21)
                    W = WStream(P, ph)
                    psr = Ring([P.ps(ph, [128, 512], F32, "psA") for _ in range(2)])
                    pss = P.ps(ph, [128, 512], F32, "pss")
                    psc = P.ps(ph, [128, 512], F32, "psc")
                    psets = Ring([[P.ps(ph, [128, 512], F32, "pst") for _ in range(4)]])
                    zqn = P.sb(ph, [128, 8, TS], BF16, "zqn")
                    zkvn = P.sb(ph, [128, 4, TS], BF16, "zkvn")
                    sqr = Ring([P.sb(ph, [128, TS], BF16, "sq") for _ in range(5)])
                    rq = P.sb(ph, [128, TS], F32, "rq")
                    rkv = P.sb(ph, [128, TS], F32, "rkv")
                    rkvc = P.sb(ph, [128, 8], F32, "rkvc")
                    stg = Ring([P.sb(ph, [128, TS], BF16, "stg") for _ in range(3)])
                    stv = Ring([P.sb(ph, [128, 4, 129], BF16, "stv") for _ in range(2)])
                    tmpf = Ring([P.sb(ph, [128, TS], F32, "tmpf") for _ in range(3)])
                    posi = P.sb(ph, [64, TS], I32, "posi")
                    posf = P.sb(ph, [64, TS], F32, "posf")
                    cosr = P.sb(ph, [64, TS], F32, "cosr")
                    sinr = P.sb(ph, [64, TS], F32, "sinr")
                    for b in stv.bufs:
                        P.op("dve", lambda e, b=b: e.memset(b.t[:], 1.0), writes=[b])
                    P.dma("sp", posi.t[:], pos_d[0:1, t0:t0 + TS].partition_broadcast(64), writes=[posi])
                    P.op("dve", lambda e: e.tensor_copy(out=posf.t[:], in_=posi.t[:]), reads=[posi], writes=[posf])
                    tA1, tB1 = tmpf.next(), tmpf.next()
                    rope_table(posf.t[:], posf, ropec.t[0:64, 2:3], 0.25, -2 * PI, cosr.t[:], cosr,
                               tA1, tA1.t[0:64, :], posi, posi.t[:], tB1, tB1.t[0:64, :])
                    rope_table(posf.t[:], posf, ropec.t[0:64, 2:3], 0.0, ropec.t[0:64, 3:4], sinr.t[:], sinr,
                               tA1, tA1.t[0:64, :], posi, posi.t[:], tB1, tB1.t[0:64, :])

                    act_h = lambda k: hT.t[:, k, :]
                    chk(22)

                    def latent(col0, nch, gt, zn, rbc, want_col):
                        sqs = []

                        def epi(blk, ps):
                            import os
                            LAT = int(os.environ.get("LAT", "9"))
                            if LAT < 2:
                                return
                            sq = sqr.next()
                            P.op("act", lambda e: e.activation(out=sq.t[:], in_=ps.t[:], func=AF.Square),
                                 reads=[ps], writes=[sq])
                            if LAT < 3:
                                return
                            P.op("dve", lambda e: e.tensor_scalar(
                                out=zn.t[:, blk, :], in0=ps.t[:], scalar1=gt.t[:, blk:blk + 1], scalar2=0.0,
                                op0=ALU.mult, op1=ALU.add), reads=[ps, gt], writes=[zn])

                            def f(e):
                                return e.matmul(pss.t[:], lhsT=ones_bf.t[:], rhs=sq.t[:], start=(blk == 0),
                                                stop=(blk == nch - 1))
                            P.op("pe", f, reads=[sq, ones_bf], writes=[pss])
                            sqs.append(sq)
                        gemm_fm(W, win_d[:, col0:col0 + nch * 128], KC, act_h, nch * 128, 128, psr, epi, [hT])
                        rsqrt_ops(pss.t[:], [pss], rbc.t[:], rbc, 1.0 / (nch * 128))
                        if want_col:
                            def fc(e):
                                for tt in range(4):
                                    for bi, sq in enumerate(sqs):
                                        ins = e.matmul(psc.t[:, tt:tt + 1], lhsT=sq.t[:, tt * 128:(tt + 1) * 128],
                                                       rhs=ones_bf.t[:, 0:1], start=(bi == 0), stop=(bi == nch - 1))
                                return ins
                            P.op("pe", fc, reads=list(sqs) + [ones_bf], writes=[psc])
                            rsqrt_ops(psc.t[:, 0:4], [psc], rkvc.t[:, 0:4], rkvc, 1.0 / (nch * 128))

                    def rope64(ps_raw, ps_perm, dstT, extra=None):
                        t1 = tmpf.next()
                        t2 = tmpf.next()
                        P.op("dve", lambda e: e.tensor_tensor(out=t1.t[0:64, :], in0=ps_raw.t[0:64, :], in1=cosr.t[:],
                                                              op=ALU.mult), reads=[ps_raw, cosr], writes=[t1])
                        P.op("dve", lambda e: e.tensor_tensor(out=t2.t[0:64, :], in0=ps_perm.t[0:64, :], in1=sinr.t[:],
                                                              op=ALU.mult), reads=[ps_perm, sinr], writes=[t2])
                        if extra is None:
                            P.op("dve", lambda e: e.tensor_tensor(out=dstT.t[0:64, :], in0=t1.t[0:64, :], in1=t2.t[0:64, :],
                                                                  op=ALU.add), reads=[t1, t2], writes=[dstT])
                        else:
                            P.op("dve", lambda e: e.tensor_tensor(out=t1.t[0:64, :], in0=t1.t[0:64, :], in1=t2.t[0:64, :],
                                                                  op=ALU.add), reads=[t1, t2], writes=[t1])
                            P.op("dve", lambda e: e.tensor_tensor(out=dstT.t[0:64, :], in0=t1.t[0:64, :],
                                                                  in1=extra.t[0:64, :], op=ALU.mult),
                                 reads=[t1, extra], writes=[dstT])

                    def store_fm(dst_ap, st, rows=128):
                        P.dma("sp", dst_ap, st.t[0:rows, :], reads=[st])

                    if kind == "own":
                        latent(C_Q, 8, qg, zqn, rq, False)
                        chk(23)
                    latent(C_KV, 4, kvg, zkvn, rkv, True)
                    chk(24)

                    hold = {}

                    def epi_kr_raw(blk, ps):
                        hold["raw"] = ps
                    gemm_fm(W, win_d[:, C_KR:C_KR + 64], KC, act_h, 64, 64, psr, epi_kr_raw, [hT])

                    def epi_kr_perm(blk, ps):
                        st = stg.next()
                        rope64(hold["raw"], ps, st)
                        store_fm(KR[:, t0:t0 + TS], st, 64)
                    gemm_fm(W, wkrp_d, KC, act_h, 64, 64, psr, epi_kr_perm, [hT])
                    chk(25)

                    def plain_store(dst_of, func=None):
                        cnt = [0]

                        def epi(blk, ps):
                            st = stg.next()
                            if func is not None:
                                P.op("act", lambda e: e.activation(out=st.t[:], in_=ps.t[:], func=func),
                                     reads=[ps], writes=[st])
                            elif cnt[0] % 2 == 0:
                                P.op("act", lambda e: e.activation(out=st.t[:], in_=ps.t[:], func=AF.Identity),
                                     reads=[ps], writes=[st])
                            else:
                                P.op("dve", lambda e: e.tensor_copy(out=st.t[:], in_=ps.t[:]), reads=[ps], writes=[st])
                            cnt[0] += 1
                            store_fm(dst_of(blk), st)
                        return epi

                    def v_epi(dst, h_base, ctile0, scale_col=None):
                        def epi(c0, cw, tt, ps):
                            st = stv.next()
                            src = ps.t[:, 0:cw].rearrange("p (h e) -> p h e", e=128)
                            nh = cw // 128
                            if scale_col is None:
                                P.op("act", lambda e: e.activation(out=st.t[:, 0:nh, 0:128], in_=src, func=AF.Identity),
                                     reads=[ps], writes=[st])
                            else:
                                P.op("act", lambda e: e.activation(out=st.t[:, 0:nh, 0:128], in_=src, func=AF.Identity,
                                                                   scale=scale_col.t[:, tt:tt + 1]),
                                     reads=[ps, scale_col], writes=[st])
                            h0 = h_base + c0 // 128
                            P.dma("sp", dst[h0:h0 + nh, :, ctile0 + tt, :].rearrange("h p e -> p h e"),
                                  st.t[:, 0:nh, :], reads=[st])
                        return epi

                    actT_h = lambda k, tt: hT.t[:, k, tt * 128:(tt + 1) * 128]
                    chk(26)
                    if kind == "own":
                        gemm_fm(W, win_d[:, C_DQ:C_DQ + 3072], KC, act_h, 3072, 128, psr,
                                plain_store(lambda blk: QD[blk, :, t0:t0 + TS]), [hT])
                    if kind in ("own", "halo"):
                        gemm_fm(W, win_d[:, C_DK:C_DK + 3072], KC, act_h, 3072, 128, psr,
                                plain_store(lambda blk: KD[blk, :, t0:t0 + TS]), [hT])
                        gemm_tm(W, win_d[:, C_DV:C_DV + 3072], KC, actT_h, 3072, psets,
                                v_epi(VD, 0, t0 // 128), [hT])
                    if kind == "own":
                        gemm_fm(W, win_d[:, C_G:C_G + 2 * D], KC, act_h, 2 * D, 128, psr,
                                plain_store(lambda blk: GT[blk, :, t0:t0 + TS], AF.Sigmoid), [hT])

                        act_q = lambda k: zqn.t[:, k, :]
                        for h in range(MLA_H):
                            qh = {}

                            def epi_n(blk, ps, h=h):
                                st = stg.next()
                                P.op("dve", lambda e: e.tensor_tensor(out=st.t[:], in0=ps.t[:], in1=rq.t[:], op=ALU.mult),
                                     reads=[ps, rq], writes=[st])
                                store_fm(QN[h, :, t0:t0 + TS], st)
                            gemm_fm(W, wuq_d[:, h * 256:h * 256 + 128], 8, act_q, 128, 128, psr, epi_n, [zqn])

                            def epi_r(blk, ps, h=h, qh=qh):
                                if blk == 0:
                                    qh["raw"] = ps
                                else:
                                    st = stg.next()
                                    rope64(qh["raw"], ps, st, extra=rq)
                                    store_fm(QR[h, :, t0:t0 + TS], st, 64)
                            gemm_fm(W, wuq_d[:, h * 256 + 128:h * 256 + 256], 8, act_q, 128, 64, psr, epi_r, [zqn])

                    act_kv = lambda k: zkvn.t[:, k, :]

                    def epi_kn(blk, ps):
                        st = stg.next()
                        P.op("dve", lambda e: e.tensor_tensor(out=st.t[:], in0=ps.t[:], in1=rkv.t[:], op=ALU.mult),
                             reads=[ps, rkv], writes=[st])
                        store_fm(KN[blk, :, t0:t0 + TS], st)
                    gemm_fm(W, wukv_d[:, 0:MLA_H * 128], 4, act_kv, MLA_H * 128, 128, psr, epi_kn, [zkvn])
                    actT_kv = lambda k, tt: zkvn.t[:, k, tt * 128:(tt + 1) * 128]
                    gemm_tm(W, wukv_d[:, MLA_H * 128:MLA_H * 256], 4, actT_kv, MLA_H * 128, psets,
                            v_epi(VA, 0, t0 // 128, scale_col=rkvc), [zkvn])
                    P.barrier()
                    chk(2)

            def attn_epilogue(O, qs_list, yrow, ytile_r, ptT_r, yst, q0, rden_r):
                for i, (Ob, Ov) in enumerate(qs_list):
                    rd = rden_r.next()
                    yt = ytile_r.next()
                    P.op("dve", lambda e, Ov=Ov, rd=rd: e.reciprocal(out=rd.t[:], in_=Ov[:, 128:129]),
                         reads=[Ob], writes=[rd])
                    P.op("act", lambda e, Ov=Ov, rd=rd, yt=yt: e.activation(
                        out=yt.t[:], in_=Ov[:, 0:128], func=AF.Identity, scale=rd.t[:, 0:1]),
                        reads=[Ob, rd], writes=[yt])
                    pt = ptT_r.next()
                    P.op("pe", lambda e, pt=pt, yt=yt: e.transpose(pt.t[:, 0:128], yt.t[:], ident.t[:]),
                         reads=[yt, ident], writes=[pt])
                    P.op("dve", lambda e, pt=pt, i=i: e.tensor_copy(out=yst.t[:, i * 128:(i + 1) * 128], in_=pt.t[:, 0:128]),
                         reads=[pt], writes=[yst])
                n = len(qs_list)
                P.dma("sp", YT[yrow, :, q0:q0 + n * 128], yst.t[:, 0:n * 128], reads=[yst])

            sc_mla = (128 + 64) ** -0.5
            with ExitStack() as ph:
                krT = P.sb(ph, [64, SEQ], BF16, "krT")
                P.dma("sp", krT.t[:], KR, writes=[krT])
                hsets = Ring([dict(qn=P.sb(ph, [128, NOWN], BF16, "qn"), qr=P.sb(ph, [64, NOWN], BF16, "qr"),
                                   kn=P.sb(ph, [128, SEQ], BF16, "kn"), va=P.sb(ph, [128, SEQ // 128, 129], BF16, "va"))
                              for _ in range(2)])
                pS = Ring([P.ps(ph, [128, 512], F32, "pS") for _ in range(3)])
                pO = Ring([P.ps(ph, [128, 4, 256], F32, "pO") for _ in range(2)])
                ptT = Ring([P.ps(ph, [128, 1024], BF16, "ptT") for _ in range(1)])
                Pr = Ring([P.sb(ph, [128, 512], BF16, "P") for _ in range(4)])
                ytr = Ring([P.sb(ph, [128, 128], BF16, "yt") for _ in range(2)])
                ystr = Ring([P.sb(ph, [128, 512], BF16, "yst") for _ in range(2)])
                rdr = Ring([P.sb(ph, [128, 1], F32, "rd") for _ in range(4)])
                NKC = SEQ // 128

                def load_head(h):
                    s = hsets.next()
                    P.dma("sp", s["qn"].t[:], QN[h], writes=[s["qn"]])
                    P.dma("sp", s["qr"].t[:], QR[h], writes=[s["qr"]])
                    P.dma("sp", s["kn"].t[:], KN[h], writes=[s["kn"]])
                    P.dma("sp", s["va"].t[:], VA[h], writes=[s["va"]])
                    return s
                nxt = load_head(0)
                for h in range(MLA_H):
                    s = nxt
                    if h + 1 < MLA_H:
                        nxt = load_head(h + 1)
                    for qg_ in range(NOWN // 512):
                        O = pO.next()

                        def emitS(kc, s=s, qg_=qg_):
                            ps = pS.next()

                            def f(e):
                                e.matmul(ps.t[:], lhsT=s["kn"].t[:, kc * 128:(kc + 1) * 128],
                                         rhs=s["qn"].t[:, qg_ * 512:(qg_ + 1) * 512], start=True, stop=False)
                                return e.matmul(ps.t[:], lhsT=krT.t[:, kc * 128:(kc + 1) * 128],
                                                rhs=s["qr"].t[:, qg_ * 512:(qg_ + 1) * 512], start=False, stop=True)
                            P.op("pe", f, reads=[s["kn"], s["qn"], s["qr"], krT], writes=[ps])
                            return ps
                        ps_next = emitS(0)
                        for kc in range(NKC):
                            ps = ps_next
                            if kc + 1 < NKC:
                                ps_next = emitS(kc + 1)
                            pb = Pr.next()
                            P.op("act", lambda e, ps=ps, pb=pb: e.activation(out=pb.t[:], in_=ps.t[:], func=AF.Exp,
                                                                             scale=sc_mla), reads=[ps], writes=[pb])

                            def f(e, pb=pb, kc=kc, O=O, s=s):
                                for qs in range(4):
                                    ins = e.matmul(O.t[:, qs, 0:129], lhsT=pb.t[:, qs * 128:(qs + 1) * 128],
                                                   rhs=s["va"].t[:, kc, :], start=(kc == 0), stop=(kc == NKC - 1))
                                return ins
                            P.op("pe", f, reads=[pb, s["va"]], writes=[O])
                        attn_epilogue(O, [(O, O.t[:, qs, :]) for qs in range(4)], h, ytr, ptT, ystr.next(),
                                      qg_ * 512, rdr)
                P.barrier()
                chk(4)

            sc_dil = 128 ** -0.5
            NT3 = NKD // 128 + 8
            with ExitStack() as ph:
                masks = P.sb(ph, [128, NSEQ, 128], BF16, "masks")
                P.dma("sp", masks.t[:].rearrange("p s i -> p (s i)"), masks_d, writes=[masks])
                cosF = P.sb(ph, [128, NKD], F32, "cosF")
                sinF = P.sb(ph, [128, NKD], F32, "sinF")
                with ExitStack() as sc:
                    pi_ = P.sb(sc, [128, 512], I32, "pi")
                    pf_ = P.sb(sc, [128, 512], F32, "pf")
                    ta_ = P.sb(sc, [128, 512], F32, "ta")
                    tb_ = P.sb(sc, [128, 512], F32, "tb")
                    ti_ = P.sb(sc, [128, 512], I32, "ti")
                    for c0 in range(0, NKD, 512):
                        P.dma("sp", pi_.t[:], pos_d[0:1, c0:c0 + 512].partition_broadcast(128), writes=[pi_])
                        P.op("dve", lambda e: e.tensor_copy(out=pf_.t[:], in_=pi_.t[:]), reads=[pi_], writes=[pf_])
                        rope_table(pf_.t[:], pf_, ropec.t[:, 0:1], 0.25, -2 * PI, cosF.t[:, c0:c0 + 512], cosF,
                                   ta_, ta_.t[:], ti_, ti_.t[:], tb_, tb_.t[:])
                        rope_table(pf_.t[:], pf_, ropec.t[:, 0:1], 0.0, ropec.t[:, 1:2], sinF.t[:, c0:c0 + 512], sinF,
                                   ta_, ta_.t[:], ti_, ti_.t[:], tb_, tb_.t[:])
                    P.barrier()
                hs3 = Ring([dict(q=[P.sb(ph, [128, NOWN], BF16, "q3") for _ in range(3)],
                                 k=[P.sb(ph, [128, NT3 * 128], BF16, "k3") for _ in range(3)],
                                 v=[P.sb(ph, [128, NT3, 129], BF16, "v3") for _ in range(3)]) for _ in range(2)])
                for s in hs3.bufs:
                    for g in range(3):
                        P.op("dve", lambda e, b=s["k"][g]: e.memset(b.t[:, 0:1024], 0.0), writes=[s["k"][g]])
                        P.op("dve", lambda e, b=s["v"][g]: e.memset(b.t[:, 0:8, :], 0.0), writes=[s["v"][g]])
                rawr = Ring([P.sb(ph, [128, NKD], BF16, "raw") for _ in range(1)])
                swpr = Ring([P.sb(ph, [128, NKD], BF16, "swp") for _ in range(1)])
                t1r = Ring([P.sb(ph, [128, NKD], F32, "t1") for _ in range(1)])
                t2r = Ring([P.sb(ph, [128, NKD], F32, "t2") for _ in range(1)])
                pS = Ring([P.ps(ph, [128, 4, 128], F32, "pS3") for _ in range(3)])
                pO = Ring([P.ps(ph, [128, 512], F32, "pO3") for _ in range(2)])
                ptT = Ring([P.ps(ph, [128, 1024], BF16, "ptT3") for _ in range(1)])
                Pr = Ring([P.sb(ph, [128, 4, 128], BF16, "P3") for _ in range(3)])
                Pm = Ring([P.sb(ph, [128, 4, 128], BF16, "Pm3") for _ in range(3)])
                ytr = Ring([P.sb(ph, [128, 128], BF16, "yt3") for _ in range(2)])
                ystr = Ring([P.sb(ph, [128, 512], BF16, "yst3") for _ in range(2)])
                rdr = Ring([P.sb(ph, [128, 1], F32, "rd3") for _ in range(4)])

                def rope_full(src, n, dst_ap, dstbuf):
                    raw = rawr.next()
                    swp = swpr.next()
                    t1 = t1r.next()
                    t2 = t2r.next()
                    P.dma("sp", raw.t[:, 0:n], src, writes=[raw])
                    P.dma("sp", swp.t[0:64, 0:n], src[64:128, :], writes=[swp])
                    P.dma("sp", swp.t[64:128, 0:n], src[0:64, :], writes=[swp])
                    P.op("dve", lambda e: e.tensor_tensor(out=t1.t[:, 0:n], in0=raw.t[:, 0:n], in1=cosF.t[:, 0:n],
                                                          op=ALU.mult), reads=[raw, cosF], writes=[t1])
                    P.op("dve", lambda e: e.tensor_tensor(out=t2.t[:, 0:n], in0=swp.t[:, 0:n], in1=sinF.t[:, 0:n],
                                                          op=ALU.mult), reads=[swp, sinF], writes=[t2])
                    P.op("dve", lambda e: e.tensor_tensor(out=dst_ap, in0=t1.t[:, 0:n], in1=t2.t[:, 0:n], op=ALU.add),
                         reads=[t1, t2], writes=[dstbuf])

                def load_head3(h):
                    s = hs3.next()
                    for g in range(3):
                        rope_full(QD[g * 8 + h], NOWN, s["q"][g].t[:], s["q"][g])
                        rope_full(KD[g * 8 + h], NKD, s["k"][g].t[:, 1024:1024 + NKD], s["k"][g])
                        P.dma("sp", s["v"][g].t[:, 8:NT3, :], VD[g * 8 + h], writes=[s["v"][g]])
                    return s
                groups = [(i, min(4, NSEQ - i)) for i in range(0, NSEQ, 4)]
                nxt = load_head3(0)
                for h in range(DIL_H):
                    s = nxt
                    if h + 1 < DIL_H:
                        nxt = load_head3(h + 1)
                    for qb in range(NOWN // 128):
                        O = pO.next()

                        def emitS(gi, s=s, qb=qb):
                            i0, n = groups[gi]
                            ps = pS.next()

                            def f(e):
                                for j in range(n):
                                    g, rel = DIL_SEQ[i0 + j]
                                    kt = qb + rel + 8
                                    ins = e.matmul(ps.t[:, j, :], lhsT=s["k"][g].t[:, kt * 128:(kt + 1) * 128],
                                                   rhs=s["q"][g].t[:, qb * 128:(qb + 1) * 128], start=True, stop=True)
                                return ins
                            P.op("pe", f, reads=s["k"] + s["q"], writes=[ps])
                            return ps
                        ps_next = emitS(0)
                        for gi, (i0, n) in enumerate(groups):
                            ps = ps_next
                            if gi + 1 < len(groups):
                                ps_next = emitS(gi + 1)
                            pb = Pr.next()
                            pm = Pm.next()
                            P.op("act", lambda e, ps=ps, pb=pb, n=n: e.activation(
                                out=pb.t[:, 0:n, :], in_=ps.t[:, 0:n, :], func=AF.Exp, scale=sc_dil),
                                reads=[ps], writes=[pb])
                            P.op("dve", lambda e, pb=pb, pm=pm, i0=i0, n=n: e.tensor_tensor(
                                out=pm.t[:, 0:n, :], in0=pb.t[:, 0:n, :], in1=masks.t[:, i0:i0 + n, :], op=ALU.mult),
                                reads=[pb, masks], writes=[pm])

                            def f(e, pm=pm, i0=i0, n=n, O=O, s=s, qb=qb):
                                for j in range(n):
                                    g, rel = DIL_SEQ[i0 + j]
                                    kt = qb + rel + 8
                                    ins = e.matmul(O.t[:, 0:129], lhsT=pm.t[:, j, :], rhs=s["v"][g].t[:, kt, :],
                                                   start=(i0 + j == 0), stop=(i0 + j == NSEQ - 1))
                                return ins
                            P.op("pe", f, reads=[pm] + s["v"], writes=[O])
                        attn_epilogue(O, [(O, O.t[:, :])], 16 + h, ytr, ptT, ystr.next(), qb * 128, rdr)
                P.barrier()
                chk(5)

            for sidx in range(NOWN // TS):
                t0 = sidx * TS
                with ExitStack() as ph:
                    W = WStream(P, ph)
                    yT = P.sb(ph, [128, 24, TS], BF16, "yT")
                    mixT = P.sb(ph, [128, KC, TS], BF16, "mixT")
                    P.dma("sp", yT.t[:], YT[:, :, t0:t0 + TS].rearrange("c p t -> p c t"), writes=[yT])
                    psA = Ring([P.ps(ph, [128, 512], F32, "psA4") for _ in range(2)])
                    psB = Ring([P.ps(ph, [128, 512], F32, "psB4") for _ in range(2)])
                    psets = Ring([[P.ps(ph, [128, 512], F32, "pst4") for _ in range(4)]])
                    gar = Ring([P.sb(ph, [128, TS], BF16, "ga") for _ in range(2)])
                    gbr = Ring([P.sb(ph, [128, TS], BF16, "gb") for _ in range(2)])
                    tmr = Ring([P.sb(ph, [128, TS], F32, "tm4") for _ in range(2)])
                    xr = Ring([P.sb(ph, [128, 512], F32, "x4") for _ in range(3)])
                    gcr = Ring([P.sb(ph, [128, 512], F32, "g1bc") for _ in range(2)])
                    holdA = {}

                    def epiA(blk, ps):
                        holdA[blk] = ps
                        ga = gar.next()
                        P.dma("sp", ga.t[:], GT[blk, :, t0:t0 + TS], writes=[ga])
                        holdA[("ga", blk)] = ga

                    def epiB(blk, ps):
                        gb = gbr.next()
                        P.dma("sp", gb.t[:], GT[NMC + blk, :, t0:t0 + TS], writes=[gb])
                        pa = holdA.pop(blk)
                        ga = holdA.pop(("ga", blk))
                        tm = tmr.next()
                        P.op("dve", lambda e: e.tensor_tensor(out=tm.t[:], in0=pa.t[:], in1=ga.t[:], op=ALU.mult),
                             reads=[pa, ga], writes=[tm])
                        tm2 = tmr.next()
                        P.op("dve", lambda e: e.tensor_tensor(out=tm2.t[:], in0=ps.t[:], in1=gb.t[:], op=ALU.mult),
                             reads=[ps, gb], writes=[tm2])
                        P.op("dve", lambda e: e.tensor_tensor(out=mixT.t[:, blk, :], in0=tm.t[:], in1=tm2.t[:], op=ALU.add),
                             reads=[tm, tm2], writes=[mixT])
                    for mc in range(NMC):
                        gemm_fm(W, wpa_d[:, mc * 128:(mc + 1) * 128], 16, lambda k: yT.t[:, k, :], 128, 128, psA,
                                lambda blk, ps, mc=mc: epiA(mc, ps), [yT])
                        gemm_fm(W, wpb_d[:, mc * 128:(mc + 1) * 128], 8, lambda k: yT.t[:, 16 + k, :], 128, 128, psB,
                                lambda blk, ps, mc=mc: epiB(mc, ps), [yT])
                    gstate = {}

                    def epi_out(c0, cw, tt, ps):
                        if tt == 0:
                            gc = gcr.next()
                            P.dma("sp", gc.t[:, 0:cw], modrow[0:1, 2 * D + c0:2 * D + c0 + cw].partition_broadcast(128),
                                  writes=[gc])
                            gstate["gc"] = gc
                        gc = gstate["gc"]
                        xt = xr.next()
                        P.dma("sp", xt.t[:, 0:cw], x_d[t0 + tt * 128:t0 + (tt + 1) * 128, c0:c0 + cw], writes=[xt])
                        tm = tmr.next()
                        P.op("dve", lambda e: e.tensor_tensor(out=tm.t[:, 0:cw], in0=ps.t[:, 0:cw], in1=gc.t[:, 0:cw],
                                                              op=ALU.mult), reads=[ps, gc], writes=[tm])
                        P.op("dve", lambda e: e.tensor_tensor(out=xt.t[:, 0:cw], in0=tm.t[:, 0:cw], in1=xt.t[:, 0:cw],
                                                              op=ALU.add), reads=[tm, xt], writes=[xt])
                        P.dma("sp", X1[t0 + tt * 128:t0 + (tt + 1) * 128, c0:c0 + cw], xt.t[:, 0:cw], reads=[xt])
                    gemm_tm(W, wout_d, KC, lambda k, tt: mixT.t[:, k, tt * 128:(tt + 1) * 128], D, psets, epi_out, [mixT])
                    P.barrier()
                    chk(6)

                with ExitStack() as ph:
                    actT = P.sb(ph, [128, NFF, TS], BF16, "actT")
                    ssq = P.sb(ph, [128, 4, 8], F32, "ssq")
                    with ExitStack() as ph2:
                        h2T = P.sb(ph2, [128, KC, TS], BF16, "h2T")
                        build_hT(ph2, lambda tt, t0=t0: X1[t0 + tt * 128:t0 + (tt + 1) * 128, :], G2, 3 * KC, h2T)
                        W = WStream(P, ph2)
                        psG = Ring([P.ps(ph2, [128, 512], F32, "psG") for _ in range(2)])
                        psU = Ring([P.ps(ph2, [128, 512], F32, "psU") for _ in range(2)])
                        sgr = Ring([P.sb(ph2, [128, TS], F32, "sg") for _ in range(5)])
                        holdG = {}

                        def epiG(blk, ps):
                            sg = sgr.next()
                            P.op("act", lambda e: e.activation(out=sg.t[:], in_=ps.t[:], func=AF.Silu),
                                 reads=[ps], writes=[sg])
                            holdG[blk] = sg

                        def epiU(blk, ps):
                            sg = holdG.pop(blk)
                            P.op("dve", lambda e: e.tensor_tensor(out=actT.t[:, blk, :], in0=ps.t[:], in1=sg.t[:],
                                                                  op=ALU.mult), reads=[ps, sg], writes=[actT])
                        fstep = max(128, min(512, (WStream.SLOT // KC) // 128 * 128))
                        for f0 in range(0, DFF, fstep):
                            fw = min(fstep, DFF - f0)
                            gemm_fm(W, wg_d[:, f0:f0 + fw], KC, lambda k: h2T.t[:, k, :], fw, 128, psG,
                                    lambda blk, ps, f0=f0: epiG(f0 // 128 + blk, ps), [h2T])
                            gemm_fm(W, wu_d[:, f0:f0 + fw], KC, lambda k: h2T.t[:, k, :], fw, 128, psU,
                                    lambda blk, ps, f0=f0: epiU(f0 // 128 + blk, ps), [h2T])
                        P.barrier()
                    with ExitStack() as ph3:
                        W = WStream(P, ph3)
                        psets = Ring([[P.ps(ph3, [128, 512], F32, "pst5") for _ in range(4)] for _ in range(2)])
                        xr = Ring([P.sb(ph3, [128, 512], F32, "x5") for _ in range(3)])
                        gcr = Ring([P.sb(ph3, [128, 512], F32, "g2bc") for _ in range(2)])
                        tmr = Ring([P.sb(ph3, [128, 512], F32, "tm5") for _ in range(2)])
                        jkr = Ring([P.sb(ph3, [128, 512], BF16, "jk5") for _ in range(2)])
                        P.op("dve", lambda e: e.memset(ssq.t[:], 0.0), writes=[ssq])
                        gstate = {}

                        def epi_dn(c0, cw, tt, ps):
                            if tt == 0:
                                gc = gcr.next()
                                P.dma("sp", gc.t[:, 0:cw],
                                      modrow[0:1, 5 * D + c0:5 * D + c0 + cw].partition_broadcast(128), writes=[gc])
                                gstate["gc"] = gc
                            gc = gstate["gc"]
                            xt = xr.next()
                            P.dma("sp", xt.t[:, 0:cw], X1[t0 + tt * 128:t0 + (tt + 1) * 128, c0:c0 + cw], writes=[xt])
                            tm = tmr.next()
                            P.op("dve", lambda e: e.tensor_tensor(out=tm.t[:, 0:cw], in0=ps.t[:, 0:cw], in1=gc.t[:, 0:cw],
                                                                  op=ALU.mult), reads=[ps, gc], writes=[tm])
                            P.op("dve", lambda e: e.tensor_tensor(out=xt.t[:, 0:cw], in0=tm.t[:, 0:cw], in1=xt.t[:, 0:cw],
                                                                  op=ALU.add), reads=[tm, xt], writes=[xt])
                            jk = jkr.next()
                            cbi = c0 // 512
                            P.op("act", lambda e: e.activation(out=jk.t[:, 0:cw], in_=xt.t[:, 0:cw], func=AF.Square,
                                                               accum_out=ssq.t[:, tt, cbi:cbi + 1]),
                                 reads=[xt], writes=[jk, ssq])
                            P.dma("sp", out_d[t0 + tt * 128:t0 + (tt + 1) * 128, c0:c0 + cw], xt.t[:, 0:cw], reads=[xt])
                        gemm_tm(W, wd_d, NFF, lambda k, tt: actT.t[:, k, tt * 128:(tt + 1) * 128], D, psets, epi_dn,
                                [actT])
                        P.barrier()
                    with ExitStack() as ph3:
                        ncb = (D + 511) // 512
                        fgb = P.sb(ph3, [128, D], F32, "fgb")
                        P.dma("sp", fgb.t[:], fg_d[0:1, :].partition_broadcast(128), writes=[fgb])
                        rowr = Ring([P.sb(ph3, [128, D], F32, "frow") for _ in range(2)])
                        rs = P.sb(ph3, [128, 8], F32, "rsf")
                        for tt in range(4):
                            row = rowr.next()
                            P.dma("sp", row.t[:], out_d[t0 + tt * 128:t0 + (tt + 1) * 128, :], writes=[row])
                            P.op("dve", lambda e, tt=tt: e.tensor_reduce(out=rs.t[:, tt:tt + 1], in_=ssq.t[:, tt, 0:ncb],
                                                                         axis=mybir.AxisListType.X, op=ALU.add),
                                 reads=[ssq], writes=[rs])
                            rsqrt_ops(rs.t[:, tt:tt + 1], [rs], rs.t[:, tt:tt + 1], rs, 1.0 / D)
                            P.op("dve", lambda e, tt=tt, row=row: e.scalar_tensor_tensor(
                                out=row.t[:], in0=row.t[:], scalar=rs.t[:, tt:tt + 1], in1=fgb.t[:],
                                op0=ALU.mult, op1=ALU.mult), reads=[row, rs, fgb], writes=[row])
                            P.dma("sp", out_d[t0 + tt * 128:t0 + (tt + 1) * 128, :], row.t[:], reads=[row])
                        P.barrier()
                        chk(7)
        except _Stop:
            pass

        with nc.Block() as block:
            @block.tensor
            def _(e):
                for t in P.q["pe"]:
                    t(e)

            @block.scalar
            def _(e):
                for t in P.q["act"]:
                    t(e)

            @block.vector
            def _(e):
                for t in P.q["dve"]:
                    t(e)

            @block.gpsimd
            def _(e):
                for t in P.q["pool"]:
                    t(e)

            @block.sync
            def _(e):
                for t in P.q["sp"]:
                    t(e)
    return nc


def host_constants():
    bf = ml_dtypes.bfloat16
    ident = np.eye(128, dtype=np.float32).astype(bf)
    j = np.arange(128)[:, None]
    i = np.arange(128)[None, :]
    masks = np.zeros((128, NSEQ, 128), np.float32)
    for s, (g, rel) in enumerate(DIL_SEQ):
        delta = rel * 128 + j - i
        masks[:, s, :] = ((delta % DIL_D[g] == 0) & (np.abs(delta) <= DIL_HALF[g])).astype(np.float32)
    masks = masks.reshape(128, NSEQ * 128).astype(bf)
    p = np.arange(128)
    ropec = np.zeros((128, 4), np.float32)
    ropec[:, 0] = THETA ** (-(2.0 * (p % 64)) / 128.0) / (2 * PI)
    ropec[:, 1] = -2 * PI * np.where(p < 64, -1.0, 1.0)
    ropec[:, 2] = THETA ** (-(2.0 * (p % 32)) / 64.0) / (2 * PI)
    ropec[:, 3] = -2 * PI * np.where(p % 64 < 32, -1.0, 1.0)
    return ident, masks, ropec


def col_layout(v, n):
    return np.ascontiguousarray(np.asarray(v, np.float32).reshape(n, 128).T)


_CACHE = {}


def run(inputs, n_batch):
    x = np.asarray(inputs["x"], np.float32)
    B, S, D = x.shape
    assert S == SEQ and B == n_batch
    DFF = inputs["w_gate"].shape[-1]
    KC = D // 128
    key = (D, DFF)
    if key not in _CACHE:
        _CACHE[key] = build_program(D, DFF)
    nc = _CACHE[key]
    ident, masks, ropec = host_constants()
    c = np.asarray(inputs["c"], np.float32)
    pos = np.asarray(inputs["positions"], np.int32)
    w_in = np.ascontiguousarray(np.asarray(inputs["w_in"], np.float32)[0])
    kr = w_in[:, 1536:1600]
    w_krp = np.ascontiguousarray(np.concatenate([kr[:, 32:64], kr[:, 0:32]], axis=1))
    wuq = np.asarray(inputs["w_uq"], np.float32)[0].reshape(QRANK, MLA_H, 192)
    wuq_r = wuq[:, :, 128:192]
    wuq2 = np.ascontiguousarray(np.concatenate(
        [wuq[:, :, 0:128], wuq_r, wuq_r[:, :, 32:64], wuq_r[:, :, 0:32]], axis=2).reshape(QRANK, MLA_H * 256))
    wukv = np.asarray(inputs["w_ukv"], np.float32)[0].reshape(KVRANK, MLA_H, 256)
    wukv2 = np.ascontiguousarray(np.concatenate(
        [wukv[:, :, 0:128].reshape(KVRANK, -1), wukv[:, :, 128:256].reshape(KVRANK, -1)], axis=1))
    shared = {
        "w_ada": np.ascontiguousarray(np.asarray(inputs["w_ada"], np.float32)[0]),
        "b_ada": np.ascontiguousarray(np.asarray(inputs["b_ada"], np.float32)[0][None, :]),
        "n1g": col_layout(inputs["norm1_g"][0], KC),
        "n2g": col_layout(inputs["norm2_g"][0], KC),
        "qg": col_layout(inputs["q_norm_g"][0], QRANK // 128),
        "kvg": col_layout(inputs["kv_norm_g"][0], KVRANK // 128),
        "fg": np.ascontiguousarray(np.asarray(inputs["final_g"], np.float32)[None, :]),
        "w_in": w_in, "w_krp": w_krp, "w_uq": wuq2, "w_ukv": wukv2,
        "w_pa": np.ascontiguousarray(np.asarray(inputs["w_proj_a"], np.float32)[0]),
        "w_pb": np.ascontiguousarray(np.asarray(inputs["w_proj_b"], np.float32)[0]),
        "w_out": np.ascontiguousarray(np.asarray(inputs["w_out"], np.float32)[0]),
        "w_gate": np.ascontiguousarray(np.asarray(inputs["w_gate"], np.float32)[0]),
        "w_up": np.ascontiguousarray(np.asarray(inputs["w_up"], np.float32)[0]),
        "w_down": np.ascontiguousarray(np.asarray(inputs["w_down"], np.float32)[0]),
        "ident": ident, "masks": masks, "ropec": ropec,
    }
    in_maps = []
    perms = []
    for b in range(B):
        for hf in range(2):
            perm = np.arange(SEQ) if hf == 0 else np.arange(SEQ - 1, -1, -1)
            perms.append(perm)
            m = dict(shared)
            m["x"] = np.ascontiguousarray(x[b][perm])
            m["pos"] = np.ascontiguousarray(pos[b][perm][None, :])
            m["c_t"] = col_layout(c[b], KC)
            in_maps.append(m)
    import os
    for alloc in nc.allocations:
        if isinstance(alloc, mybir.MemoryLocationSet) and alloc.kind == "ExternalInput" and alloc.tensor_shape is not None:
            nm = alloc.memorylocations[0].name
            if nm in in_maps[0]:
                a = in_maps[0][nm]
                if tuple(a.shape) != tuple(alloc.tensor_shape) or a.dtype != mybir.dt.np(alloc.dtype):
                    print("MISMATCH", nm, a.shape, a.dtype, alloc.tensor_shape, alloc.dtype)
            else:
                print("MISSING", nm)
    if os.environ.get("K1CORE"):
        in_maps = in_maps[:1]
        res = run_bass_kernel_spmd(nc, in_maps, core_ids=[0])
        o = res.results[0]["out"]
        global LAST
        LAST = res.results[0]
        out = np.zeros((B, S, D), np.float32)
        out[0][perms[0][:NOWN]] = o
        return out
    res = run_bass_kernel_spmd(nc, in_maps, core_ids=list(range(2 * B)))
    out = np.empty((B, S, D), np.float32)
    for b in range(B):
        for hf in range(2):
            o = res.results[b * 2 + hf]["out"]
            out[b][perms[b * 2 + hf][:NOWN]] = o
    return out


def kernel(**inputs):
    return run(inputs, 4)
```

```python
import math
from contextlib import ExitStack

import numpy as np
import ml_dtypes

import concourse.bass as bass
import concourse.mybir as mybir
from concourse.bass_utils import run_bass_kernel_spmd

F32 = mybir.dt.float32
BF16 = mybir.dt.bfloat16
I32 = mybir.dt.int32
AF = mybir.ActivationFunctionType
ALU = mybir.AluOpType

SEQ = 4096
NOWN = 2048
NHALO = 1024
TS = 512
MLA_H = 16
DIL_H = 8
DIL_G = 3
QRANK = 1024
KVRANK = 512
EPS = 1e-6
THETA = 10000.0
PI = math.pi
NDS = 8
ENG = ("pe", "act", "dve", "pool", "sp")

DIL_SEQ = [(0, r) for r in (-1, 0, 1)] + [(1, r) for r in range(-2, 3)] + [(2, r) for r in range(-8, 9)]
DIL_D = (1, 4, 16)
DIL_HALF = (64, 256, 1024)
NSEQ = len(DIL_SEQ)


class Buf:
    __slots__ = ("t", "w", "r", "x")

    def __init__(self, t, x=False):
        self.t = t
        self.w = None
        self.r = {}
        self.x = x


class Prog:
    def __init__(self, nc, es):
        self.nc = nc
        self.es = es
        self.q = {e: [] for e in ENG}
        self.prog = {e: es.enter_context(nc.semaphore("pg_" + e)) for e in ENG}
        self.cnt = {e: 0 for e in ENG}
        self.waited = {}
        self.dsem = {q: [es.enter_context(nc.semaphore("d%s%d" % (q, i))) for i in range(NDS)]
                     for q in ("sp", "pool")}
        self.dcnt = {q: [0] * NDS for q in ("sp", "pool")}
        self.dnext = {q: 0 for q in ("sp", "pool")}
        self.uid = 0
        self.dead = False

    def _wait(self, eng, tok):
        if tok is None or self.dead:
            return
        sem, val, src, key = tok
        if src == eng and eng == "pe":
            return
        wk = (eng, key)
        if self.waited.get(wk, 0) >= val:
            return
        self.waited[wk] = val
        self.q[eng].append(lambda e: e.wait_ge(sem, val))

    def _deps(self, eng, reads, writes):
        for b in reads:
            self._wait(eng, b.w)
            if b.x:
                for t in b.r.values():
                    self._wait(eng, t)
        for b in writes:
            self._wait(eng, b.w)
            for t in b.r.values():
                self._wait(eng, t)

    def _mark(self, tok, reads, writes):
        for b in reads:
            b.r[tok[3]] = tok
        for b in writes:
            b.w = tok
            b.r = {}

    def op(self, eng, fn, reads=(), writes=()):
        if self.dead:
            return None
        self._deps(eng, reads, writes)
        self.cnt[eng] += 1
        sem = self.prog[eng]
        tok = (sem, self.cnt[eng], eng, "pg_" + eng)
        self.q[eng].append(lambda e: fn(e).then_inc(sem, 1))
        self._mark(tok, reads, writes)
        return tok

    def dma(self, q, out, in_, reads=(), writes=()):
        if self.dead:
            return None
        i = self.dnext[q]
        self.dnext[q] = (i + 1) % NDS
        sem = self.dsem[q][i]
        key = "d%s%d" % (q, i)
        if self.dcnt[q][i] > 0:
            self._wait(q, (sem, self.dcnt[q][i], "dma", key))
        self._deps(q, reads, writes)
        self.dcnt[q][i] += 16
        tok = (sem, self.dcnt[q][i], "dma", key)
        self.q[q].append(lambda e: e.dma_start(out=out, in_=in_).then_inc(sem, 16))
        self._mark(tok, reads, writes)
        return tok

    def barrier(self):
        if self.dead:
            return
        toks = [(self.prog[e], self.cnt[e], e, "pg_" + e) for e in ENG if self.cnt[e] > 0]
        for q in ("sp", "pool"):
            for i in range(NDS):
                if self.dcnt[q][i] > 0:
                    toks.append((self.dsem[q][i], self.dcnt[q][i], "dma", "d%s%d" % (q, i)))
        for e in ENG:
            for t in toks:
                if t[2] == e:
                    continue
                self._wait(e, t)

    def sb(self, es, shape, dt, name=None):
        self.uid += 1
        return Buf(es.enter_context(self.nc.sbuf_tensor("%s_%d" % (name or "sb", self.uid), list(shape), dt)))

    def ps(self, es, shape, dt, name=None):
        self.uid += 1
        return Buf(es.enter_context(self.nc.psum_tensor("%s_%d" % (name or "ps", self.uid), list(shape), dt)), True)


class Ring:
    def __init__(self, bufs):
        self.bufs = bufs
        self.i = 0

    def next(self):
        b = self.bufs[self.i]
        self.i = (self.i + 1) % len(self.bufs)
        return b


class WStream:
    SLOT = 8192

    def __init__(self, P, es, n=4):
        self.P = P
        self.ring = Ring([P.sb(es, [128, self.SLOT], BF16, "w") for _ in range(n)])

    def get(self, wap, kb, cw, kp=128):
        b = self.ring.next()
        view = b.t[0:kp, 0:kb * cw].rearrange("p (k c) -> p k c", k=kb)
        self.P.dma("pool", view, wap.rearrange("(k p) c -> p k c", p=kp), writes=[b])
        return b, view


def kgroups(n, g=16):
    out = []
    k = 0
    while k < n:
        out.append((k, min(g, n - k)))
        k += g
    return out


class _Stop(Exception):
    pass


def build_program(D, DFF, stop=0):
    KC = D // 128
    NFF = DFF // 128
    NMC = D // 128
    nc = bass.Bass("TRN2", target_bir_lowering=False)

    def din(name, shape, dt=F32):
        return nc.dram_tensor(name, list(shape), dt, kind="ExternalInput").ap()

    import os
    _dbg = bool(os.environ.get("KDBG"))

    def dscr(name, shape, dt=BF16):
        return nc.dram_tensor(name, list(shape), dt, kind="ExternalOutput" if _dbg else "Internal").ap()

    x_d = din("x", [SEQ, D])
    pos_d = din("pos", [1, SEQ], I32)
    c_d = din("c_t", [128, KC])
    wada_d = din("w_ada", [D, 6 * D])
    bada_d = din("b_ada", [1, 6 * D])
    n1g_d = din("n1g", [128, KC])
    n2g_d = din("n2g", [128, KC])
    qg_d = din("qg", [128, QRANK // 128])
    kvg_d = din("kvg", [128, KVRANK // 128])
    fg_d = din("fg", [1, D])
    win_d = din("w_in", [D, 1600 + 9216 + 2 * D])
    wkrp_d = din("w_krp", [D, 64])
    wuq_d = din("w_uq", [QRANK, MLA_H * 256])
    wukv_d = din("w_ukv", [KVRANK, MLA_H * 256])
    wpa_d = din("w_pa", [MLA_H * 128, D])
    wpb_d = din("w_pb", [DIL_H * 128, D])
    wout_d = din("w_out", [D, D])
    wg_d = din("w_gate", [D, DFF])
    wu_d = din("w_up", [D, DFF])
    wd_d = din("w_down", [DFF, D])
    ident_d = din("ident", [128, 128], BF16)
    masks_d = din("masks", [128, NSEQ * 128], BF16)
    rope_d = din("ropec", [128, 4])
    out_d = nc.dram_tensor("out", [NOWN, D], F32, kind="ExternalOutput").ap()

    modrow = dscr("modrow", [1, 6 * D], F32)
    KR = dscr("KR", [64, SEQ])
    QN = dscr("QN", [MLA_H, 128, NOWN])
    QR = dscr("QR", [MLA_H, 64, NOWN])
    KN = dscr("KN", [MLA_H, 128, SEQ])
    VA = dscr("VA", [MLA_H, 128, SEQ // 128, 129])
    NKD = NOWN + NHALO
    QD = dscr("QD", [24, 128, NOWN])
    KD = dscr("KD", [24, 128, NKD])
    VD = dscr("VD", [24, 128, NKD // 128, 129])
    GT = dscr("GT", [2 * NMC, 128, NOWN])
    YT = dscr("YT", [24, 128, NOWN])
    X1 = dscr("X1", [NOWN, D], F32)

    C_Q, C_KV, C_KR, C_DQ = 0, 1024, 1536, 1600
    C_DK, C_DV, C_G = 1600 + 3072, 1600 + 6144, 1600 + 9216

    with ExitStack() as es:
        P = Prog(nc, es)
        ident = P.sb(es, [128, 128], BF16, "ident")
        ones_bf = P.sb(es, [128, 128], BF16, "ones")
        ones_f = P.sb(es, [1, 128], F32, "onesf")
        modT = P.sb(es, [128, 6 * KC], F32, "modT")
        G1 = P.sb(es, [128, KC], F32, "G1")
        G2 = P.sb(es, [128, KC], F32, "G2")
        n1g = P.sb(es, [128, KC], F32, "n1g")
        n2g = P.sb(es, [128, KC], F32, "n2g")
        qg = P.sb(es, [128, 8], F32, "qg")
        kvg = P.sb(es, [128, 4], F32, "kvg")
        ropec = P.sb(es, [128, 4], F32, "ropec")
        consts = [ident, ones_bf, ones_f, modT, G1, G2, n1g, n2g, qg, kvg, ropec]

        P.dma("sp", ident.t[:], ident_d, writes=[ident])
        P.dma("sp", n1g.t[:], n1g_d, writes=[n1g])
        P.dma("sp", n2g.t[:], n2g_d, writes=[n2g])
        P.dma("sp", qg.t[:], qg_d, writes=[qg])
        P.dma("sp", kvg.t[:], kvg_d, writes=[kvg])
        P.dma("sp", ropec.t[:], rope_d, writes=[ropec])
        P.op("dve", lambda e: e.memset(ones_bf.t[:], 1.0), writes=[ones_bf])
        P.op("dve", lambda e: e.memset(ones_f.t[:], 1.0), writes=[ones_f])

        if stop == -1:
            P.barrier()
            P.dead = True

        def chk(n):
            if stop == n:
                P.barrier()
                P.dead = True

        try:
            if stop == -2:
                P.barrier()
                raise _Stop()
            with ExitStack() as ph:
                W = WStream(P, ph)
                cf = P.sb(ph, [128, KC], F32, "cf")
                cact = P.sb(ph, [128, KC], BF16, "cact")
                rowr = Ring([P.sb(ph, [1, 512], F32, "row") for _ in range(2)])
                brr = Ring([P.sb(ph, [1, 512], F32, "brow") for _ in range(2)])
                pmr = Ring([P.ps(ph, [128, 512], F32, "pm") for _ in range(2)])
                ptr = Ring([P.ps(ph, [128, 512], F32, "pt") for _ in range(2)])
                P.dma("sp", cf.t[:], c_d, writes=[cf])
                P.op("act", lambda e: e.activation(out=cact.t[:], in_=cf.t[:], func=AF.Silu),
                     reads=[cf], writes=[cact])
                for cb in range(6 * D // 512):
                    pm = pmr.next()
                    for (k0, kb) in kgroups(KC):
                        wb, wv = W.get(wada_d[k0 * 128:(k0 + kb) * 128, cb * 512:(cb + 1) * 512], kb, 512)

                        def f(e, k0=k0, kb=kb, wv=wv, pm=pm):
                            for k in range(kb):
                                ins = e.matmul(pm.t[0:1, :], lhsT=cact.t[:, k0 + k:k0 + k + 1], rhs=wv[:, k, :],
                                               start=(k0 + k == 0), stop=(k0 + k == KC - 1))
                            return ins
                        P.op("pe", f, reads=[wb, cact], writes=[pm])
                    br = brr.next()
                    row = rowr.next()
                    P.dma("sp", br.t[:], bada_d[0:1, cb * 512:(cb + 1) * 512], writes=[br])
                    P.op("dve", lambda e, row=row, pm=pm, br=br: e.tensor_tensor(
                        out=row.t[:], in0=pm.t[0:1, :], in1=br.t[:], op=ALU.add), reads=[pm, br], writes=[row])
                    P.dma("sp", modrow[0:1, cb * 512:(cb + 1) * 512], row.t[:], reads=[row])
                    pt = ptr.next()

                    def f2(e, row=row, pt=pt):
                        for j in range(4):
                            ins = e.matmul(pt.t[:, j:j + 1], lhsT=row.t[0:1, j * 128:(j + 1) * 128],
                                           rhs=ones_f.t[0:1, 0:1], start=True, stop=True)
                        return ins
                    P.op("pe", f2, reads=[row, ones_f], writes=[pt])
                    P.op("dve", lambda e, pt=pt, cb=cb: e.tensor_copy(out=modT.t[:, cb * 4:cb * 4 + 4], in_=pt.t[:, 0:4]),
                         reads=[pt], writes=[modT])
                P.op("dve", lambda e: e.scalar_tensor_tensor(out=G1.t[:], in0=modT.t[:, KC:2 * KC], scalar=1.0,
                                                             in1=n1g.t[:], op0=ALU.add, op1=ALU.mult),
                     reads=[modT, n1g], writes=[G1])
                P.op("dve", lambda e: e.scalar_tensor_tensor(out=G2.t[:], in0=modT.t[:, 4 * KC:5 * KC], scalar=1.0,
                                                             in1=n2g.t[:], op0=ALU.add, op1=ALU.mult),
                     reads=[modT, n2g], writes=[G2])
                P.barrier()
                chk(1)

            def rsqrt_ops(src_ap, src_bufs, dst_ap, dst_buf, scale):
                P.op("act", lambda e: e.activation(out=dst_ap, in_=src_ap, func=AF.Sqrt, scale=scale, bias=EPS),
                     reads=src_bufs, writes=[dst_buf])
                P.op("dve", lambda e: e.reciprocal(out=dst_ap, in_=dst_ap), reads=[dst_buf], writes=[dst_buf])

            def rope_table(posf_ap, posf_buf, inv_ap, phase, scale, dst_ap, dst_buf, tA, tA_ap, tI, tI_ap, tB, tB_ap):
                P.op("dve", lambda e: e.tensor_scalar(out=tA_ap, in0=posf_ap, scalar1=inv_ap, scalar2=phase,
                                                      op0=ALU.mult, op1=ALU.add), reads=[posf_buf, ropec], writes=[tA])
                P.op("dve", lambda e: e.tensor_copy(out=tI_ap, in_=tA_ap), reads=[tA], writes=[tI])
                P.op("dve", lambda e: e.tensor_copy(out=tB_ap, in_=tI_ap), reads=[tI], writes=[tB])
                P.op("dve", lambda e: e.tensor_tensor(out=tB_ap, in0=tA_ap, in1=tB_ap, op=ALU.subtract),
                     reads=[tA, tB], writes=[tB])
                P.op("dve", lambda e: e.scalar_tensor_tensor(out=tA_ap, in0=tB_ap, scalar=0.5, in1=tB_ap,
                                                             op0=ALU.is_gt, op1=ALU.subtract),
                     reads=[tB], writes=[tA])
                P.op("act", lambda e: e.activation(out=dst_ap, in_=tA_ap, func=AF.Sin, scale=scale),
                     reads=[tA, ropec], writes=[dst_buf])

            def build_hT(ph, src_rows, Gt, sh_col0, hT):
                with ExitStack() as sc:
                    xr = Ring([P.sb(sc, [128, D], F32, "xt") for _ in range(2)])
                    xsr = Ring([P.sb(sc, [128, D], BF16, "xs") for _ in range(2)])
                    ssr = Ring([P.sb(sc, [128, 2], F32, "ss") for _ in range(2)])
                    ptr = Ring([P.ps(sc, [128, 8, 128], BF16, "ptT") for _ in range(3)])
                    for tt in range(TS // 128):
                        xt = xr.next()
                        xs = xsr.next()
                        ss = ssr.next()
                        P.dma("sp", xt.t[:], src_rows(tt), writes=[xt])
                        P.op("dve", lambda e, ss=ss: e.memset(ss.t[:], 0.0), writes=[ss])
                        P.op("act", lambda e, xt=xt, xs=xs, ss=ss: e.activation(
                            out=xs.t[:], in_=xt.t[:], func=AF.Square, accum_out=ss.t[:, 0:1]),
                            reads=[xt], writes=[xs, ss])
                        rsqrt_ops(ss.t[:, 0:1], [ss], ss.t[:, 1:2], ss, 1.0 / D)
                        P.op("act", lambda e, xt=xt, xs=xs, ss=ss: e.activation(
                            out=xs.t[:], in_=xt.t[:], func=AF.Identity, scale=ss.t[:, 1:2]),
                            reads=[xt, ss], writes=[xs])
                        for k0 in range(0, KC, 8):
                            kn = min(8, KC - k0)
                            pt = ptr.next()

                            def f(e, k0=k0, kn=kn, pt=pt, xs=xs):
                                for j in range(kn):
                                    ins = e.transpose(pt.t[:, j, :], xs.t[:, (k0 + j) * 128:(k0 + j + 1) * 128], ident.t[:])
                                return ins
                            P.op("pe", f, reads=[xs, ident], writes=[pt])
                            for j in range(kn):
                                kc = k0 + j
                                if kc % 2 == 0:
                                    P.op("act", lambda e, pt=pt, j=j, kc=kc, tt=tt: e.activation(
                                        out=hT.t[:, kc, tt * 128:(tt + 1) * 128], in_=pt.t[:, j, :], func=AF.Identity,
                                        scale=Gt.t[:, kc:kc + 1], bias=modT.t[:, sh_col0 + kc:sh_col0 + kc + 1]),
                                        reads=[pt, Gt, modT], writes=[hT])
                                else:
                                    P.op("dve", lambda e, pt=pt, j=j, kc=kc, tt=tt: e.tensor_scalar(
                                        out=hT.t[:, kc, tt * 128:(tt + 1) * 128], in0=pt.t[:, j, :],
                                        scalar1=Gt.t[:, kc:kc + 1], scalar2=modT.t[:, sh_col0 + kc:sh_col0 + kc + 1],
                                        op0=ALU.mult, op1=ALU.add), reads=[pt, Gt, modT], writes=[hT])
                    P.barrier()

            def gemm_fm(W, wap, kch, act_of, ncols, M, psr, epi, act_bufs, kp=128):
                cbmax = max(M, min(512, (WStream.SLOT // kch) // M * M))
                for c0 in range(0, ncols, cbmax):
                    cw = min(cbmax, ncols - c0)
                    wb, wv = W.get(wap[:, c0:c0 + cw], kch, cw, kp)
                    for m0 in range(0, cw, M):
                        ps = psr.next()

                        def f(e, wv=wv, m0=m0, ps=ps):
                            for k in range(kch):
                                ins = e.matmul(ps.t[0:M, :], lhsT=wv[:, k, m0:m0 + M], rhs=act_of(k),
                                               start=(k == 0), stop=(k == kch - 1))
                            return ins
                        P.op("pe", f, reads=[wb] + act_bufs, writes=[ps])
                        epi((c0 + m0) // M, ps)

            def gemm_tm(W, wap, kch, actT_of, ncols, psets, epi, act_bufs):
                kg = kgroups(kch)
                for c0 in range(0, ncols, 512):
                    cw = min(512, ncols - c0)
                    pset = psets.next()
                    for (k0, kb) in kg:
                        wb, wv = W.get(wap[k0 * 128:(k0 + kb) * 128, c0:c0 + cw], kb, cw)

                        def f(e, wv=wv, k0=k0, kb=kb, pset=pset, cw=cw):
                            for tt in range(4):
                                for k in range(kb):
                                    ins = e.matmul(pset[tt].t[:, 0:cw], lhsT=actT_of(k0 + k, tt), rhs=wv[:, k, :],
                                                   start=(k0 + k == 0), stop=(k0 + k == kch - 1))
                            return ins
                        P.op("pe", f, reads=[wb] + act_bufs, writes=list(pset))
                    for tt in range(4):
                        epi(c0, cw, tt, pset[tt])

            slabs = [("own", s * TS) for s in range(NOWN // TS)] + \
                    [("halo", NOWN + s * TS) for s in range(NHALO // TS)] + \
                    [("far", NOWN + NHALO + s * TS) for s in range((SEQ - NOWN - NHALO) // TS)]
            for (kind, t0) in slabs:
                with ExitStack() as ph:
                    hT = P.sb(ph, [128, KC, TS], BF16, "hT")
                    build_hT(ph, lambda tt, t0=t0: x_d[t0 + tt * 128:t0 + (tt + 1) * 128, :], G1, 0, hT)
                    chk(21)
                    W = WStream(P, ph)
                    psr = Ring([P.ps(ph, [128, 512], F32, "psA") for _ in range(2)])
                    pss = P.ps(ph, [128, 512], F32, "pss")
                    psc = P.ps(ph, [128, 512], F32, "psc")
                    psets = Ring([[P.ps(ph, [128, 512], F32, "pst") for _ in range(4)]])
                    zqn = P.sb(ph, [128, 8, TS], BF16, "zqn")
                    zkvn = P.sb(ph, [128, 4, TS], BF16, "zkvn")
                    sqr = Ring([P.sb(ph, [128, TS], BF16, "sq") for _ in range(5)])
                    rq = P.sb(ph, [128, TS], F32, "rq")
                    rkv = P.sb(ph, [128, TS], F32, "rkv")
                    rkvc = P.sb(ph, [128, 8], F32, "rkvc")
                    stg = Ring([P.sb(ph, [128, TS], BF16, "stg") for _ in range(3)])
                    stv = Ring([P.sb(ph, [128, 4, 129], BF16, "stv") for _ in range(2)])
                    tmpf = Ring([P.sb(ph, [128, TS], F32, "tmpf") for _ in range(3)])
                    posi = P.sb(ph, [64, TS], I32, "posi")
                    posf = P.sb(ph, [64, TS], F32, "posf")
                    cosr = P.sb(ph, [64, TS], F32, "cosr")
                    sinr = P.sb(ph, [64, TS], F32, "sinr")
                    for b in stv.bufs:
                        P.op("dve", lambda e, b=b: e.memset(b.t[:], 1.0), writes=[b])
                    P.dma("sp", posi.t[:], pos_d[0:1, t0:t0 + TS].partition_broadcast(64), writes=[posi])
                    P.op("dve", lambda e: e.tensor_copy(out=posf.t[:], in_=posi.t[:]), reads=[posi], writes=[posf])
                    tA1, tB1 = tmpf.next(), tmpf.next()
                    rope_table(posf.t[:], posf, ropec.t[0:64, 2:3], 0.25, -2 * PI, cosr.t[:], cosr,
                               tA1, tA1.t[0:64, :], posi, posi.t[:], tB1, tB1.t[0:64, :])
                    rope_table(posf.t[:], posf, ropec.t[0:64, 2:3], 0.0, ropec.t[0:64, 3:4], sinr.t[:], sinr,
                               tA1, tA1.t[0:64, :], posi, posi.t[:], tB1, tB1.t[0:64, :])

                    act_h = lambda k: hT.t[:, k, :]
                    chk(22)

                    def latent(col0, nch, gt, zn, rbc, want_col):
                        sqs = []

                        def epi(blk, ps):
                            import os
                            LAT = int(os.environ.get("LAT", "9"))
                            if LAT < 2:
                                return
                            sq = sqr.next()
                            P.op("act", lambda e: e.activation(out=sq.t[:], in_=ps.t[:], func=AF.Square),
                                 reads=[ps], writes=[sq])
                            if LAT < 3:
                                return
                            P.op("dve", lambda e: e.tensor_scalar(
                                out=zn.t[:, blk, :], in0=ps.t[:], scalar1=gt.t[:, blk:blk + 1], scalar2=0.0,
                                op0=ALU.mult, op1=ALU.add), reads=[ps, gt], writes=[zn])

                            def f(e):
                                return e.matmul(pss.t[:], lhsT=ones_bf.t[:], rhs=sq.t[:], start=(blk == 0),
                                                stop=(blk == nch - 1))
                            P.op("pe", f, reads=[sq, ones_bf], writes=[pss])
                            sqs.append(sq)
                        gemm_fm(W, win_d[:, col0:col0 + nch * 128], KC, act_h, nch * 128, 128, psr, epi, [hT])
                        rsqrt_ops(pss.t[:], [pss], rbc.t[:], rbc, 1.0 / (nch * 128))
                        if want_col:
                            def fc(e):
                                for tt in range(4):
                                    for bi, sq in enumerate(sqs):
                                        ins = e.matmul(psc.t[:, tt:tt + 1], lhsT=sq.t[:, tt * 128:(tt + 1) * 128],
                                                       rhs=ones_bf.t[:, 0:1], start=(bi == 0), stop=(bi == nch - 1))
                                return ins
                            P.op("pe", fc, reads=list(sqs) + [ones_bf], writes=[psc])
                            rsqrt_ops(psc.t[:, 0:4], [psc], rkvc.t[:, 0:4], rkvc, 1.0 / (nch * 128))

                    def rope64(ps_raw, ps_perm, dstT, extra=None):
                        t1 = tmpf.next()
                        t2 = tmpf.next()
                        P.op("dve", lambda e: e.tensor_tensor(out=t1.t[0:64, :], in0=ps_raw.t[0:64, :], in1=cosr.t[:],
                                                              op=ALU.mult), reads=[ps_raw, cosr], writes=[t1])
                        P.op("dve", lambda e: e.tensor_tensor(out=t2.t[0:64, :], in0=ps_perm.t[0:64, :], in1=sinr.t[:],
                                                              op=ALU.mult), reads=[ps_perm, sinr], writes=[t2])
                        if extra is None:
                            P.op("dve", lambda e: e.tensor_tensor(out=dstT.t[0:64, :], in0=t1.t[0:64, :], in1=t2.t[0:64, :],
                                                                  op=ALU.add), reads=[t1, t2], writes=[dstT])
                        else:
                            P.op("dve", lambda e: e.tensor_tensor(out=t1.t[0:64, :], in0=t1.t[0:64, :], in1=t2.t[0:64, :],
                                                                  op=ALU.add), reads=[t1, t2], writes=[t1])
                            P.op("dve", lambda e: e.tensor_tensor(out=dstT.t[0:64, :], in0=t1.t[0:64, :],
                                                                  in1=extra.t[0:64, :], op=ALU.mult),
                                 reads=[t1, extra], writes=[dstT])

                    def store_fm(dst_ap, st, rows=128):
                        P.dma("sp", dst_ap, st.t[0:rows, :], reads=[st])

                    if kind == "own":
                        latent(C_Q, 8, qg, zqn, rq, False)
                        chk(23)
                    latent(C_KV, 4, kvg, zkvn, rkv, True)
                    chk(24)

                    hold = {}

                    def epi_kr_raw(blk, ps):
                        hold["raw"] = ps
                    gemm_fm(W, win_d[:, C_KR:C_KR + 64], KC, act_h, 64, 64, psr, epi_kr_raw, [hT])

                    def epi_kr_perm(blk, ps):
                        st = stg.next()
                        rope64(hold["raw"], ps, st)
                        store_fm(KR[:, t0:t0 + TS], st, 64)
                    gemm_fm(W, wkrp_d, KC, act_h, 64, 64, psr, epi_kr_perm, [hT])
                    chk(25)

                    def plain_store(dst_of, func=None):
                        cnt = [0]

                        def epi(blk, ps):
                            st = stg.next()
                            if func is not None:
                                P.op("act", lambda e: e.activation(out=st.t[:], in_=ps.t[:], func=func),
                                     reads=[ps], writes=[st])
                            elif cnt[0] % 2 == 0:
                                P.op("act", lambda e: e.activation(out=st.t[:], in_=ps.t[:], func=AF.Identity),
                                     reads=[ps], writes=[st])
                            else:
                                P.op("dve", lambda e: e.tensor_copy(out=st.t[:], in_=ps.t[:]), reads=[ps], writes=[st])
                            cnt[0] += 1
                            store_fm(dst_of(blk), st)
                        return epi

                    def v_epi(dst, h_base, ctile0, scale_col=None):
                        def epi(c0, cw, tt, ps):
                            st = stv.next()
                            src = ps.t[:, 0:cw].rearrange("p (h e) -> p h e", e=128)
                            nh = cw // 128
                            if scale_col is None:
                                P.op("act", lambda e: e.activation(out=st.t[:, 0:nh, 0:128], in_=src, func=AF.Identity),
                                     reads=[ps], writes=[st])
                            else:
                                P.op("act", lambda e: e.activation(out=st.t[:, 0:nh, 0:128], in_=src, func=AF.Identity,
                                                                   scale=scale_col.t[:, tt:tt + 1]),
                                     reads=[ps, scale_col], writes=[st])
                            h0 = h_base + c0 // 128
                            P.dma("sp", dst[h0:h0 + nh, :, ctile0 + tt, :].rearrange("h p e -> p h e"),
                                  st.t[:, 0:nh, :], reads=[st])
                        return epi

                    actT_h = lambda k, tt: hT.t[:, k, tt * 128:(tt + 1) * 128]
                    chk(26)
                    if kind == "own":
                        gemm_fm(W, win_d[:, C_DQ:C_DQ + 3072], KC, act_h, 3072, 128, psr,
                                plain_store(lambda blk: QD[blk, :, t0:t0 + TS]), [hT])
                    if kind in ("own", "halo"):
                        gemm_fm(W, win_d[:, C_DK:C_DK + 3072], KC, act_h, 3072, 128, psr,
                                plain_store(lambda blk: KD[blk, :, t0:t0 + TS]), [hT])
                        gemm_tm(W, win_d[:, C_DV:C_DV + 3072], KC, actT_h, 3072, psets,
                                v_epi(VD, 0, t0 // 128), [hT])
                    if kind == "own":
                        gemm_fm(W, win_d[:, C_G:C_G + 2 * D], KC, act_h, 2 * D, 128, psr,
                                plain_store(lambda blk: GT[blk, :, t0:t0 + TS], AF.Sigmoid), [hT])

                        act_q = lambda k: zqn.t[:, k, :]
                        for h in range(MLA_H):
                            qh = {}

                            def epi_n(blk, ps, h=h):
                                st = stg.next()
                                P.op("dve", lambda e: e.tensor_tensor(out=st.t[:], in0=ps.t[:], in1=rq.t[:], op=ALU.mult),
                                     reads=[ps, rq], writes=[st])
                                store_fm(QN[h, :, t0:t0 + TS], st)
                            gemm_fm(W, wuq_d[:, h * 256:h * 256 + 128], 8, act_q, 128, 128, psr, epi_n, [zqn])

                            def epi_r(blk, ps, h=h, qh=qh):
                                if blk == 0:
                                    qh["raw"] = ps
                                else:
                                    st = stg.next()
                                    rope64(qh["raw"], ps, st, extra=rq)
                                    store_fm(QR[h, :, t0:t0 + TS], st, 64)
                            gemm_fm(W, wuq_d[:, h * 256 + 128:h * 256 + 256], 8, act_q, 128, 64, psr, epi_r, [zqn])

                    act_kv = lambda k: zkvn.t[:, k, :]

                    def epi_kn(blk, ps):
                        st = stg.next()
                        P.op("dve", lambda e: e.tensor_tensor(out=st.t[:], in0=ps.t[:], in1=rkv.t[:], op=ALU.mult),
                             reads=[ps, rkv], writes=[st])
                        store_fm(KN[blk, :, t0:t0 + TS], st)
                    gemm_fm(W, wukv_d[:, 0:MLA_H * 128], 4, act_kv, MLA_H * 128, 128, psr, epi_kn, [zkvn])
                    actT_kv = lambda k, tt: zkvn.t[:, k, tt * 128:(tt + 1) * 128]
                    gemm_tm(W, wukv_d[:, MLA_H * 128:MLA_H * 256], 4, actT_kv, MLA_H * 128, psets,
                            v_epi(VA, 0, t0 // 128, scale_col=rkvc), [zkvn])
                    P.barrier()
                    chk(2)

            def attn_epilogue(O, qs_list, yrow, ytile_r, ptT_r, yst, q0, rden_r):
                for i, (Ob, Ov) in enumerate(qs_list):
                    rd = rden_r.next()
                    yt = ytile_r.next()
                    P.op("dve", lambda e, Ov=Ov, rd=rd: e.reciprocal(out=rd.t[:], in_=Ov[:, 128:129]),
                         reads=[Ob], writes=[rd])
                    P.op("act", lambda e, Ov=Ov, rd=rd, yt=yt: e.activation(
                        out=yt.t[:], in_=Ov[:, 0:128], func=AF.Identity, scale=rd.t[:, 0:1]),
                        reads=[Ob, rd], writes=[yt])
                    pt = ptT_r.next()
                    P.op("pe", lambda e, pt=pt, yt=yt: e.transpose(pt.t[:, 0:128], yt.t[:], ident.t[:]),
                         reads=[yt, ident], writes=[pt])
                    P.op("dve", lambda e, pt=pt, i=i: e.tensor_copy(out=yst.t[:, i * 128:(i + 1) * 128], in_=pt.t[:, 0:128]),
                         reads=[pt], writes=[yst])
                n = len(qs_list)
                P.dma("sp", YT[yrow, :, q0:q0 + n * 128], yst.t[:, 0:n * 128], reads=[yst])

            sc_mla = (128 + 64) ** -0.5
            with ExitStack() as ph:
                krT = P.sb(ph, [64, SEQ], BF16, "krT")
                P.dma("sp", krT.t[:], KR, writes=[krT])
                hsets = Ring([dict(qn=P.sb(ph, [128, NOWN], BF16, "qn"), qr=P.sb(ph, [64, NOWN], BF16, "qr"),
                                   kn=P.sb(ph, [128, SEQ], BF16, "kn"), va=P.sb(ph, [128, SEQ // 128, 129], BF16, "va"))
                              for _ in range(2)])
                pS = Ring([P.ps(ph, [128, 512], F32, "pS") for _ in range(3)])
                pO = Ring([[P.ps(ph, [128, 512], F32, "pOn"), P.ps(ph, [128, 512], F32, "pOd")] for _ in range(2)])
                Pr = Ring([P.sb(ph, [128, 512], BF16, "P") for _ in range(4)])
                ystr = Ring([P.sb(ph, [128, 512], BF16, "yst") for _ in range(2)])
                rdr = Ring([P.sb(ph, [128, 512], F32, "rd") for _ in range(2)])
                NKC = SEQ // 128

                def load_head(h):
                    s = hsets.next()
                    P.dma("sp", s["qn"].t[:], QN[h], writes=[s["qn"]])
                    P.dma("sp", s["qr"].t[:], QR[h], writes=[s["qr"]])
                    P.dma("sp", s["kn"].t[:], KN[h], writes=[s["kn"]])
                    P.dma("sp", s["va"].t[:], VA[h], writes=[s["va"]])
                    return s
                nxt = load_head(0)
                for h in range(MLA_H):
                    s = nxt
                    if h + 1 < MLA_H:
                        nxt = load_head(h + 1)
                    for qg_ in range(NOWN // 512):
                        On, Od = pO.next()

                        def emitS(kc, s=s, qg_=qg_):
                            ps = pS.next()

                            def f(e):
                                e.matmul(ps.t[:], lhsT=s["kn"].t[:, kc * 128:(kc + 1) * 128],
                                         rhs=s["qn"].t[:, qg_ * 512:(qg_ + 1) * 512], start=True, stop=False)
                                return e.matmul(ps.t[:], lhsT=krT.t[:, kc * 128:(kc + 1) * 128],
                                                rhs=s["qr"].t[:, qg_ * 512:(qg_ + 1) * 512], start=False, stop=True)
                            P.op("pe", f, reads=[s["kn"], s["qn"], s["qr"], krT], writes=[ps])
                            return ps
                        ps_next = emitS(0)
                        for kc in range(NKC):
                            ps = ps_next
                            if kc + 1 < NKC:
                                ps_next = emitS(kc + 1)
                            pb = Pr.next()
                            P.op("act", lambda e, ps=ps, pb=pb: e.activation(out=pb.t[:], in_=ps.t[:], func=AF.Exp,
                                                                             scale=sc_mla), reads=[ps], writes=[pb])

                            def f(e, pb=pb, kc=kc, On=On, Od=Od, s=s):
                                e.matmul(On.t[:], lhsT=s["va"].t[:, kc, 0:128], rhs=pb.t[:],
                                         start=(kc == 0), stop=(kc == NKC - 1))
                                return e.matmul(Od.t[:], lhsT=ones_bf.t[:], rhs=pb.t[:],
                                                start=(kc == 0), stop=(kc == NKC - 1))
                            P.op("pe", f, reads=[pb, s["va"], ones_bf], writes=[On, Od])
                        rd = rdr.next()
                        yst = ystr.next()
                        P.op("dve", lambda e, Od=Od, rd=rd: e.reciprocal(out=rd.t[:], in_=Od.t[:]),
                             reads=[Od], writes=[rd])
                        P.op("dve", lambda e, On=On, rd=rd, yst=yst: e.tensor_tensor(
                            out=yst.t[:], in0=On.t[:], in1=rd.t[:], op=ALU.mult), reads=[On, rd], writes=[yst])
                        P.dma("sp", YT[h, :, qg_ * 512:(qg_ + 1) * 512], yst.t[:], reads=[yst])
                P.barrier()
                chk(4)

            sc_dil = 128 ** -0.5
            NT3 = NKD // 128 + 8
            with ExitStack() as ph:
                masks = P.sb(ph, [128, NSEQ, 128], BF16, "masks")
                P.dma("sp", masks.t[:].rearrange("p s i -> p (s i)"), masks_d, writes=[masks])
                cosF = P.sb(ph, [128, NKD], F32, "cosF")
                sinF = P.sb(ph, [128, NKD], F32, "sinF")
                with ExitStack() as sc:
                    pi_ = P.sb(sc, [128, 512], I32, "pi")
                    pf_ = P.sb(sc, [128, 512], F32, "pf")
                    ta_ = P.sb(sc, [128, 512], F32, "ta")
                    tb_ = P.sb(sc, [128, 512], F32, "tb")
                    ti_ = P.sb(sc, [128, 512], I32, "ti")
                    for c0 in range(0, NKD, 512):
                        P.dma("sp", pi_.t[:], pos_d[0:1, c0:c0 + 512].partition_broadcast(128), writes=[pi_])
                        P.op("dve", lambda e: e.tensor_copy(out=pf_.t[:], in_=pi_.t[:]), reads=[pi_], writes=[pf_])
                        rope_table(pf_.t[:], pf_, ropec.t[:, 0:1], 0.25, -2 * PI, cosF.t[:, c0:c0 + 512], cosF,
                                   ta_, ta_.t[:], ti_, ti_.t[:], tb_, tb_.t[:])
                        rope_table(pf_.t[:], pf_, ropec.t[:, 0:1], 0.0, ropec.t[:, 1:2], sinF.t[:, c0:c0 + 512], sinF,
                                   ta_, ta_.t[:], ti_, ti_.t[:], tb_, tb_.t[:])
                    P.barrier()
                hs3 = Ring([dict(q=[P.sb(ph, [128, NOWN], BF16, "q3") for _ in range(3)],
                                 k=[P.sb(ph, [128, NT3 * 128], BF16, "k3") for _ in range(3)],
                                 v=[P.sb(ph, [128, NT3, 129], BF16, "v3") for _ in range(3)]) for _ in range(2)])
                for s in hs3.bufs:
                    for g in range(3):
                        P.op("dve", lambda e, b=s["k"][g]: e.memset(b.t[:, 0:1024], 0.0), writes=[s["k"][g]])
                        P.op("dve", lambda e, b=s["v"][g]: e.memset(b.t[:, 0:8, :], 0.0), writes=[s["v"][g]])
                rawr = Ring([P.sb(ph, [128, NKD], BF16, "raw") for _ in range(1)])
                swpr = Ring([P.sb(ph, [128, NKD], BF16, "swp") for _ in range(1)])
                t1r = Ring([P.sb(ph, [128, NKD], F32, "t1") for _ in range(1)])
                t2r = Ring([P.sb(ph, [128, NKD], F32, "t2") for _ in range(1)])
                pS = Ring([P.ps(ph, [128, 4, 128], F32, "pS3") for _ in range(3)])
                pO = Ring([P.ps(ph, [128, 512], F32, "pO3") for _ in range(2)])
                ptT = Ring([P.ps(ph, [128, 1024], BF16, "ptT3") for _ in range(1)])
                Pr = Ring([P.sb(ph, [128, 4, 128], BF16, "P3") for _ in range(3)])
                Pm = Ring([P.sb(ph, [128, 4, 128], BF16, "Pm3") for _ in range(3)])
                ytr = Ring([P.sb(ph, [128, 128], BF16, "yt3") for _ in range(2)])
                ystr = Ring([P.sb(ph, [128, 512], BF16, "yst3") for _ in range(2)])
                rdr = Ring([P.sb(ph, [128, 1], F32, "rd3") for _ in range(4)])

                def rope_full(src, n, dst_ap, dstbuf):
                    raw = rawr.next()
                    swp = swpr.next()
                    t1 = t1r.next()
                    t2 = t2r.next()
                    P.dma("sp", raw.t[:, 0:n], src, writes=[raw])
                    P.dma("sp", swp.t[0:64, 0:n], src[64:128, :], writes=[swp])
                    P.dma("sp", swp.t[64:128, 0:n], src[0:64, :], writes=[swp])
                    P.op("dve", lambda e: e.tensor_tensor(out=t1.t[:, 0:n], in0=raw.t[:, 0:n], in1=cosF.t[:, 0:n],
                                                          op=ALU.mult), reads=[raw, cosF], writes=[t1])
                    P.op("dve", lambda e: e.tensor_tensor(out=t2.t[:, 0:n], in0=swp.t[:, 0:n], in1=sinF.t[:, 0:n],
                                                          op=ALU.mult), reads=[swp, sinF], writes=[t2])
                    P.op("dve", lambda e: e.tensor_tensor(out=dst_ap, in0=t1.t[:, 0:n], in1=t2.t[:, 0:n], op=ALU.add),
                         reads=[t1, t2], writes=[dstbuf])

                def load_head3(h):
                    s = hs3.next()
                    for g in range(3):
                        rope_full(QD[g * 8 + h], NOWN, s["q"][g].t[:], s["q"][g])
                        rope_full(KD[g * 8 + h], NKD, s["k"][g].t[:, 1024:1024 + NKD], s["k"][g])
                        P.dma("sp", s["v"][g].t[:, 8:NT3, :], VD[g * 8 + h], writes=[s["v"][g]])
                    return s
                groups = [(i, min(4, NSEQ - i)) for i in range(0, NSEQ, 4)]
                nxt = load_head3(0)
                for h in range(DIL_H):
                    s = nxt
                    if h + 1 < DIL_H:
                        nxt = load_head3(h + 1)
                    for qb in range(NOWN // 128):
                        O = pO.next()

                        def emitS(gi, s=s, qb=qb):
                            i0, n = groups[gi]
                            ps = pS.next()

                            def f(e):
                                for j in range(n):
                                    g, rel = DIL_SEQ[i0 + j]
                                    kt = qb + rel + 8
                                    ins = e.matmul(ps.t[:, j, :], lhsT=s["k"][g].t[:, kt * 128:(kt + 1) * 128],
                                                   rhs=s["q"][g].t[:, qb * 128:(qb + 1) * 128], start=True, stop=True)
                                return ins
                            P.op("pe", f, reads=s["k"] + s["q"], writes=[ps])
                            return ps
                        ps_next = emitS(0)
                        for gi, (i0, n) in enumerate(groups):
                            ps = ps_next
                            if gi + 1 < len(groups):
                                ps_next = emitS(gi + 1)
                            pb = Pr.next()
                            pm = Pm.next()
                            P.op("act", lambda e, ps=ps, pb=pb, n=n: e.activation(
                                out=pb.t[:, 0:n, :], in_=ps.t[:, 0:n, :], func=AF.Exp, scale=sc_dil),
                                reads=[ps], writes=[pb])
                            P.op("dve", lambda e, pb=pb, pm=pm, i0=i0, n=n: e.tensor_tensor(
                                out=pm.t[:, 0:n, :], in0=pb.t[:, 0:n, :], in1=masks.t[:, i0:i0 + n, :], op=ALU.mult),
                                reads=[pb, masks], writes=[pm])

                            def f(e, pm=pm, i0=i0, n=n, O=O, s=s, qb=qb):
                                for j in range(n):
                                    g, rel = DIL_SEQ[i0 + j]
                                    kt = qb + rel + 8
                                    ins = e.matmul(O.t[:, 0:129], lhsT=pm.t[:, j, :], rhs=s["v"][g].t[:, kt, :],
                                                   start=(i0 + j == 0), stop=(i0 + j == NSEQ - 1))
                                return ins
                            P.op("pe", f, reads=[pm] + s["v"], writes=[O])
                        attn_epilogue(O, [(O, O.t[:, :])], 16 + h, ytr, ptT, ystr.next(), qb * 128, rdr)
                P.barrier()
                chk(5)

            for sidx in range(NOWN // TS):
                t0 = sidx * TS
                with ExitStack() as ph:
                    W = WStream(P, ph)
                    yT = P.sb(ph, [128, 24, TS], BF16, "yT")
                    mixT = P.sb(ph, [128, KC, TS], BF16, "mixT")
                    P.dma("sp", yT.t[:], YT[:, :, t0:t0 + TS].rearrange("c p t -> p c t"), writes=[yT])
                    psA = Ring([P.ps(ph, [128, 512], F32, "psA4") for _ in range(2)])
                    psB = Ring([P.ps(ph, [128, 512], F32, "psB4") for _ in range(2)])
                    psets = Ring([[P.ps(ph, [128, 512], F32, "pst4") for _ in range(4)]])
                    gar = Ring([P.sb(ph, [128, TS], BF16, "ga") for _ in range(2)])
                    gbr = Ring([P.sb(ph, [128, TS], BF16, "gb") for _ in range(2)])
                    tmr = Ring([P.sb(ph, [128, TS], F32, "tm4") for _ in range(2)])
                    xr = Ring([P.sb(ph, [128, 512], F32, "x4") for _ in range(3)])
                    gcr = Ring([P.sb(ph, [128, 512], F32, "g1bc") for _ in range(2)])
                    holdA = {}

                    def epiA(blk, ps):
                        holdA[blk] = ps
                        ga = gar.next()
                        P.dma("sp", ga.t[:], GT[blk, :, t0:t0 + TS], writes=[ga])
                        holdA[("ga", blk)] = ga

                    def epiB(blk, ps):
                        gb = gbr.next()
                        P.dma("sp", gb.t[:], GT[NMC + blk, :, t0:t0 + TS], writes=[gb])
                        pa = holdA.pop(blk)
                        ga = holdA.pop(("ga", blk))
                        tm = tmr.next()
                        P.op("dve", lambda e: e.tensor_tensor(out=tm.t[:], in0=pa.t[:], in1=ga.t[:], op=ALU.mult),
                             reads=[pa, ga], writes=[tm])
                        tm2 = tmr.next()
                        P.op("dve", lambda e: e.tensor_tensor(out=tm2.t[:], in0=ps.t[:], in1=gb.t[:], op=ALU.mult),
                             reads=[ps, gb], writes=[tm2])
                        P.op("dve", lambda e: e.tensor_tensor(out=mixT.t[:, blk, :], in0=tm.t[:], in1=tm2.t[:], op=ALU.add),
                             reads=[tm, tm2], writes=[mixT])
                    for mc in range(NMC):
                        gemm_fm(W, wpa_d[:, mc * 128:(mc + 1) * 128], 16, lambda k: yT.t[:, k, :], 128, 128, psA,
                                lambda blk, ps, mc=mc: epiA(mc, ps), [yT])
                        gemm_fm(W, wpb_d[:, mc * 128:(mc + 1) * 128], 8, lambda k: yT.t[:, 16 + k, :], 128, 128, psB,
                                lambda blk, ps, mc=mc: epiB(mc, ps), [yT])
                    gstate = {}

                    def epi_out(c0, cw, tt, ps):
                        if tt == 0:
                            gc = gcr.next()
                            P.dma("sp", gc.t[:, 0:cw], modrow[0:1, 2 * D + c0:2 * D + c0 + cw].partition_broadcast(128),
                                  writes=[gc])
                            gstate["gc"] = gc
                        gc = gstate["gc"]
                        xt = xr.next()
                        P.dma("sp", xt.t[:, 0:cw], x_d[t0 + tt * 128:t0 + (tt + 1) * 128, c0:c0 + cw], writes=[xt])
                        tm = tmr.next()
                        P.op("dve", lambda e: e.tensor_tensor(out=tm.t[:, 0:cw], in0=ps.t[:, 0:cw], in1=gc.t[:, 0:cw],
                                                              op=ALU.mult), reads=[ps, gc], writes=[tm])
                        P.op("dve", lambda e: e.tensor_tensor(out=xt.t[:, 0:cw], in0=tm.t[:, 0:cw], in1=xt.t[:, 0:cw],
                                                              op=ALU.add), reads=[tm, xt], writes=[xt])
                        P.dma("sp", X1[t0 + tt * 128:t0 + (tt + 1) * 128, c0:c0 + cw], xt.t[:, 0:cw], reads=[xt])
                    gemm_tm(W, wout_d, KC, lambda k, tt: mixT.t[:, k, tt * 128:(tt + 1) * 128], D, psets, epi_out, [mixT])
                    P.barrier()
                    chk(6)

                with ExitStack() as ph:
                    actT = P.sb(ph, [128, NFF, TS], BF16, "actT")
                    ssq = P.sb(ph, [128, 4, 8], F32, "ssq")
                    with ExitStack() as ph2:
                        h2T = P.sb(ph2, [128, KC, TS], BF16, "h2T")
                        build_hT(ph2, lambda tt, t0=t0: X1[t0 + tt * 128:t0 + (tt + 1) * 128, :], G2, 3 * KC, h2T)
                        W = WStream(P, ph2)
                        psG = Ring([P.ps(ph2, [128, 512], F32, "psG") for _ in range(2)])
                        psU = Ring([P.ps(ph2, [128, 512], F32, "psU") for _ in range(2)])
                        sgr = Ring([P.sb(ph2, [128, TS], F32, "sg") for _ in range(5)])
                        holdG = {}

                        def epiG(blk, ps):
                            sg = sgr.next()
                            P.op("act", lambda e: e.activation(out=sg.t[:], in_=ps.t[:], func=AF.Silu),
                                 reads=[ps], writes=[sg])
                            holdG[blk] = sg

                        def epiU(blk, ps):
                            sg = holdG.pop(blk)
                            P.op("dve", lambda e: e.tensor_tensor(out=actT.t[:, blk, :], in0=ps.t[:], in1=sg.t[:],
                                                                  op=ALU.mult), reads=[ps, sg], writes=[actT])
                        fstep = max(128, min(512, (WStream.SLOT // KC) // 128 * 128))
                        for f0 in range(0, DFF, fstep):
                            fw = min(fstep, DFF - f0)
                            gemm_fm(W, wg_d[:, f0:f0 + fw], KC, lambda k: h2T.t[:, k, :], fw, 128, psG,
                                    lambda blk, ps, f0=f0: epiG(f0 // 128 + blk, ps), [h2T])
                            gemm_fm(W, wu_d[:, f0:f0 + fw], KC, lambda k: h2T.t[:, k, :], fw, 128, psU,
                                    lambda blk, ps, f0=f0: epiU(f0 // 128 + blk, ps), [h2T])
                        P.barrier()
                    with ExitStack() as ph3:
                        W = WStream(P, ph3)
                        psets = Ring([[P.ps(ph3, [128, 512], F32, "pst5") for _ in range(4)] for _ in range(2)])
                        xr = Ring([P.sb(ph3, [128, 512], F32, "x5") for _ in range(3)])
                        gcr = Ring([P.sb(ph3, [128, 512], F32, "g2bc") for _ in range(2)])
                        tmr = Ring([P.sb(ph3, [128, 512], F32, "tm5") for _ in range(2)])
                        jkr = Ring([P.sb(ph3, [128, 512], BF16, "jk5") for _ in range(2)])
                        P.op("dve", lambda e: e.memset(ssq.t[:], 0.0), writes=[ssq])
                        gstate = {}

                        def epi_dn(c0, cw, tt, ps):
                            if tt == 0:
                                gc = gcr.next()
                                P.dma("sp", gc.t[:, 0:cw],
                                      modrow[0:1, 5 * D + c0:5 * D + c0 + cw].partition_broadcast(128), writes=[gc])
                                gstate["gc"] = gc
                            gc = gstate["gc"]
                            xt = xr.next()
                            P.dma("sp", xt.t[:, 0:cw], X1[t0 + tt * 128:t0 + (tt + 1) * 128, c0:c0 + cw], writes=[xt])
                            tm = tmr.next()
                            P.op("dve", lambda e: e.tensor_tensor(out=tm.t[:, 0:cw], in0=ps.t[:, 0:cw], in1=gc.t[:, 0:cw],
                                                                  op=ALU.mult), reads=[ps, gc], writes=[tm])
                            P.op("dve", lambda e: e.tensor_tensor(out=xt.t[:, 0:cw], in0=tm.t[:, 0:cw], in1=xt.t[:, 0:cw],
                                                                  op=ALU.add), reads=[tm, xt], writes=[xt])
                            jk = jkr.next()
                            cbi = c0 // 512
                            P.op("act", lambda e: e.activation(out=jk.t[:, 0:cw], in_=xt.t[:, 0:cw], func=AF.Square,
                                                               accum_out=ssq.t[:, tt, cbi:cbi + 1]),
                                 reads=[xt], writes=[jk, ssq])
                            P.dma("sp", out_d[t0 + tt * 128:t0 + (tt + 1) * 128, c0:c0 + cw], xt.t[:, 0:cw], reads=[xt])
                        gemm_tm(W, wd_d, NFF, lambda k, tt: actT.t[:, k, tt * 128:(tt + 1) * 128], D, psets, epi_dn,
                                [actT])
                        P.barrier()
                    with ExitStack() as ph3:
                        ncb = (D + 511) // 512
                        fgb = P.sb(ph3, [128, D], F32, "fgb")
                        P.dma("sp", fgb.t[:], fg_d[0:1, :].partition_broadcast(128), writes=[fgb])
                        rowr = Ring([P.sb(ph3, [128, D], F32, "frow") for _ in range(2)])
                        rs = P.sb(ph3, [128, 8], F32, "rsf")
                        for tt in range(4):
                            row = rowr.next()
                            P.dma("sp", row.t[:], out_d[t0 + tt * 128:t0 + (tt + 1) * 128, :], writes=[row])
                            P.op("dve", lambda e, tt=tt: e.tensor_reduce(out=rs.t[:, tt:tt + 1], in_=ssq.t[:, tt, 0:ncb],
                                                                         axis=mybir.AxisListType.X, op=ALU.add),
                                 reads=[ssq], writes=[rs])
                            rsqrt_ops(rs.t[:, tt:tt + 1], [rs], rs.t[:, tt:tt + 1], rs, 1.0 / D)
                            P.op("dve", lambda e, tt=tt, row=row: e.scalar_tensor_tensor(
                                out=row.t[:], in0=row.t[:], scalar=rs.t[:, tt:tt + 1], in1=fgb.t[:],
                                op0=ALU.mult, op1=ALU.mult), reads=[row, rs, fgb], writes=[row])
                            P.dma("sp", out_d[t0 + tt * 128:t0 + (tt + 1) * 128, :], row.t[:], reads=[row])
                        P.barrier()
                        chk(7)
        except _Stop:
            pass

        with nc.Block() as block:
            @block.tensor
            def _(e):
                for t in P.q["pe"]:
                    t(e)

            @block.scalar
            def _(e):
                for t in P.q["act"]:
                    t(e)

            @block.vector
            def _(e):
                for t in P.q["dve"]:
                    t(e)

            @block.gpsimd
            def _(e):
                for t in P.q["pool"]:
                    t(e)

            @block.sync
            def _(e):
                for t in P.q["sp"]:
                    t(e)
    return nc


def host_constants():
    bf = ml_dtypes.bfloat16
    ident = np.eye(128, dtype=np.float32).astype(bf)
    j = np.arange(128)[:, None]
    i = np.arange(128)[None, :]
    masks = np.zeros((128, NSEQ, 128), np.float32)
    for s, (g, rel) in enumerate(DIL_SEQ):
        delta = rel * 128 + j - i
        masks[:, s, :] = ((delta % DIL_D[g] == 0) & (np.abs(delta) <= DIL_HALF[g])).astype(np.float32)
    masks = masks.reshape(128, NSEQ * 128).astype(bf)
    p = np.arange(128)
    ropec = np.zeros((128, 4), np.float32)
    ropec[:, 0] = THETA ** (-(2.0 * (p % 64)) / 128.0) / (2 * PI)
    ropec[:, 1] = -2 * PI * np.where(p < 64, -1.0, 1.0)
    ropec[:, 2] = THETA ** (-(2.0 * (p % 32)) / 64.0) / (2 * PI)
    ropec[:, 3] = -2 * PI * np.where(p % 64 < 32, -1.0, 1.0)
    return ident, masks, ropec


def col_layout(v, n):
    return np.ascontiguousarray(np.asarray(v, np.float32).reshape(n, 128).T)


_CACHE = {}


def run(inputs, n_batch):
    x = np.asarray(inputs["x"], np.float32)
    B, S, D = x.shape
    assert S == SEQ and B == n_batch
    DFF = inputs["w_gate"].shape[-1]
    KC = D // 128
    key = (D, DFF)
    if key not in _CACHE:
        _CACHE[key] = build_program(D, DFF)
    nc = _CACHE[key]
    ident, masks, ropec = host_constants()
    c = np.asarray(inputs["c"], np.float32)
    pos = np.asarray(inputs["positions"], np.int32)
    w_in = np.ascontiguousarray(np.asarray(inputs["w_in"], np.float32)[0])
    kr = w_in[:, 1536:1600]
    w_krp = np.ascontiguousarray(np.concatenate([kr[:, 32:64], kr[:, 0:32]], axis=1))
    wuq = np.asarray(inputs["w_uq"], np.float32)[0].reshape(QRANK, MLA_H, 192)
    wuq_r = wuq[:, :, 128:192]
    wuq2 = np.ascontiguousarray(np.concatenate(
        [wuq[:, :, 0:128], wuq_r, wuq_r[:, :, 32:64], wuq_r[:, :, 0:32]], axis=2).reshape(QRANK, MLA_H * 256))
    wukv = np.asarray(inputs["w_ukv"], np.float32)[0].reshape(KVRANK, MLA_H, 256)
    wukv2 = np.ascontiguousarray(np.concatenate(
        [wukv[:, :, 0:128].reshape(KVRANK, -1), wukv[:, :, 128:256].reshape(KVRANK, -1)], axis=1))
    shared = {
        "w_ada": np.ascontiguousarray(np.asarray(inputs["w_ada"], np.float32)[0]),
        "b_ada": np.ascontiguousarray(np.asarray(inputs["b_ada"], np.float32)[0][None, :]),
        "n1g": col_layout(inputs["norm1_g"][0], KC),
        "n2g": col_layout(inputs["norm2_g"][0], KC),
        "qg": col_layout(inputs["q_norm_g"][0], QRANK // 128),
        "kvg": col_layout(inputs["kv_norm_g"][0], KVRANK // 128),
        "fg": np.ascontiguousarray(np.asarray(inputs["final_g"], np.float32)[None, :]),
        "w_in": w_in, "w_krp": w_krp, "w_uq": wuq2, "w_ukv": wukv2,
        "w_pa": np.ascontiguousarray(np.asarray(inputs["w_proj_a"], np.float32)[0]),
        "w_pb": np.ascontiguousarray(np.asarray(inputs["w_proj_b"], np.float32)[0]),
        "w_out": np.ascontiguousarray(np.asarray(inputs["w_out"], np.float32)[0]),
        "w_gate": np.ascontiguousarray(np.asarray(inputs["w_gate"], np.float32)[0]),
        "w_up": np.ascontiguousarray(np.asarray(inputs["w_up"], np.float32)[0]),
        "w_down": np.ascontiguousarray(np.asarray(inputs["w_down"], np.float32)[0]),
        "ident": ident, "masks": masks, "ropec": ropec,
    }
    in_maps = []
    perms = []
    for b in range(B):
        for hf in range(2):
            perm = np.arange(SEQ) if hf == 0 else np.arange(SEQ - 1, -1, -1)
            perms.append(perm)
            m = dict(shared)
            m["x"] = np.ascontiguousarray(x[b][perm])
            m["pos"] = np.ascontiguousarray(pos[b][perm][None, :])
            m["c_t"] = col_layout(c[b], KC)
            in_maps.append(m)
    import os
    for alloc in nc.allocations:
        if isinstance(alloc, mybir.MemoryLocationSet) and alloc.kind == "ExternalInput" and alloc.tensor_shape is not None:
            nm = alloc.memorylocations[0].name
            if nm in in_maps[0]:
                a = in_maps[0][nm]
                if tuple(a.shape) != tuple(alloc.tensor_shape) or a.dtype != mybir.dt.np(alloc.dtype):
                    print("MISMATCH", nm, a.shape, a.dtype, alloc.tensor_shape, alloc.dtype)
            else:
                print("MISSING", nm)
    if os.environ.get("K1CORE"):
        in_maps = in_maps[:1]
        res = run_bass_kernel_spmd(nc, in_maps, core_ids=[0])
        o = res.results[0]["out"]
        global LAST
        LAST = res.results[0]
        out = np.zeros((B, S, D), np.float32)
        out[0][perms[0][:NOWN]] = o
        return out
    res = run_bass_kernel_spmd(nc, in_maps, core_ids=list(range(2 * B)))
    out = np.empty((B, S, D), np.float32)
    for b in range(B):
        for hf in range(2):
            o = res.results[b * 2 + hf]["out"]
            out[b][perms[b * 2 + hf][:NOWN]] = o
    return out


def kernel(**inputs):
    return run(inputs, 4)
```

```python
import math
from contextlib import ExitStack

import numpy as np
import ml_dtypes

import concourse.bass as bass
import concourse.mybir as mybir
from concourse.bass_utils import run_bass_kernel_spmd

F32 = mybir.dt.float32
BF16 = mybir.dt.bfloat16
I32 = mybir.dt.int32
AF = mybir.ActivationFunctionType
ALU = mybir.AluOpType

SEQ = 4096
NOWN = 2048
NHALO = 1024
TS = 512
MLA_H = 16
DIL_H = 8
DIL_G = 3
QRANK = 1024
KVRANK = 512
EPS = 1e-6
THETA = 10000.0
PI = math.pi
NDS = 8
ENG = ("pe", "act", "dve", "pool", "sp")

DIL_SEQ = [(0, r) for r in (-1, 0, 1)] + [(1, r) for r in range(-2, 3)] + [(2, r) for r in range(-8, 9)]
DIL_D = (1, 4, 16)
DIL_HALF = (64, 256, 1024)
NSEQ = len(DIL_SEQ)


class Buf:
    __slots__ = ("t", "w", "r", "x")

    def __init__(self, t, x=False):
        self.t = t
        self.w = None
        self.r = {}
        self.x = x


class Prog:
    def __init__(self, nc, es):
        self.nc = nc
        self.es = es
        self.q = {e: [] for e in ENG}
        self.prog = {e: es.enter_context(nc.semaphore("pg_" + e)) for e in ENG}
        self.cnt = {e: 0 for e in ENG}
        self.waited = {}
        self.dsem = {q: [es.enter_context(nc.semaphore("d%s%d" % (q, i))) for i in range(NDS)]
                     for q in ("sp", "pool")}
        self.dcnt = {q: [0] * NDS for q in ("sp", "pool")}
        self.dnext = {q: 0 for q in ("sp", "pool")}
        self.uid = 0
        self.dead = False

    def _wait(self, eng, tok):
        if tok is None or self.dead:
            return
        sem, val, src, key = tok
        if src == eng and eng == "pe":
            return
        wk = (eng, key)
        if self.waited.get(wk, 0) >= val:
            return
        self.waited[wk] = val
        self.q[eng].append(lambda e: e.wait_ge(sem, val))

    def _deps(self, eng, reads, writes):
        for b in reads:
            self._wait(eng, b.w)
            if b.x:
                for t in b.r.values():
                    self._wait(eng, t)
        for b in writes:
            self._wait(eng, b.w)
            for t in b.r.values():
                self._wait(eng, t)

    def _mark(self, tok, reads, writes):
        for b in reads:
            b.r[tok[3]] = tok
        for b in writes:
            b.w = tok
            b.r = {}

    def op(self, eng, fn, reads=(), writes=()):
        if self.dead:
            return None
        self._deps(eng, reads, writes)
        self.cnt[eng] += 1
        sem = self.prog[eng]
        tok = (sem, self.cnt[eng], eng, "pg_" + eng)
        self.q[eng].append(lambda e: fn(e).then_inc(sem, 1))
        self._mark(tok, reads, writes)
        return tok

    def dma(self, q, out, in_, reads=(), writes=()):
        if self.dead:
            return None
        i = self.dnext[q]
        self.dnext[q] = (i + 1) % NDS
        sem = self.dsem[q][i]
        key = "d%s%d" % (q, i)
        if self.dcnt[q][i] > 0:
            self._wait(q, (sem, self.dcnt[q][i], "dma", key))
        self._deps(q, reads, writes)
        self.dcnt[q][i] += 16
        tok = (sem, self.dcnt[q][i], "dma", key)
        self.q[q].append(lambda e: e.dma_start(out=out, in_=in_).then_inc(sem, 16))
        self._mark(tok, reads, writes)
        return tok

    def barrier(self):
        if self.dead:
            return
        toks = [(self.prog[e], self.cnt[e], e, "pg_" + e) for e in ENG if self.cnt[e] > 0]
        for q in ("sp", "pool"):
            for i in range(NDS):
                if self.dcnt[q][i] > 0:
                    toks.append((self.dsem[q][i], self.dcnt[q][i], "dma", "d%s%d" % (q, i)))
        for e in ENG:
            if e == "pool":
                continue
            for t in toks:
                if t[2] == e:
                    continue
                self._wait(e, t)

    def sb(self, es, shape, dt, name=None):
        self.uid += 1
        return Buf(es.enter_context(self.nc.sbuf_tensor("%s_%d" % (name or "sb", self.uid), list(shape), dt)))

    def ps(self, es, shape, dt, name=None):
        self.uid += 1
        return Buf(es.enter_context(self.nc.psum_tensor("%s_%d" % (name or "ps", self.uid), list(shape), dt)), True)


class Ring:
    def __init__(self, bufs):
        self.bufs = bufs
        self.i = 0

    def next(self):
        b = self.bufs[self.i]
        self.i = (self.i + 1) % len(self.bufs)
        return b


class WStream:
    SLOT = 8192

    def __init__(self, P, es, n=4):
        self.P = P
        self.ring = Ring([P.sb(es, [128, self.SLOT], BF16, "w") for _ in range(n)])

    def get(self, wap, kb, cw, kp=128):
        b = self.ring.next()
        view = b.t[0:kp, 0:kb * cw].rearrange("p (k c) -> p k c", k=kb)
        self.P.dma("pool", view, wap.rearrange("(k p) c -> p k c", p=kp), writes=[b])
        return b, view


def kgroups(n, g=16):
    out = []
    k = 0
    while k < n:
        out.append((k, min(g, n - k)))
        k += g
    return out


class _Stop(Exception):
    pass


def build_program(D, DFF, stop=0):
    KC = D // 128
    NFF = DFF // 128
    NMC = D // 128
    nc = bass.Bass("TRN2", target_bir_lowering=False)

    def din(name, shape, dt=F32):
        return nc.dram_tensor(name, list(shape), dt, kind="ExternalInput").ap()

    import os
    _dbg = bool(os.environ.get("KDBG"))

    def dscr(name, shape, dt=BF16):
        return nc.dram_tensor(name, list(shape), dt, kind="ExternalOutput" if _dbg else "Internal").ap()

    x_d = din("x", [SEQ, D])
    pos_d = din("pos", [1, SEQ], I32)
    c_d = din("c_t", [128, KC])
    wada_d = din("w_ada", [D, 6 * D])
    bada_d = din("b_ada", [1, 6 * D])
    n1g_d = din("n1g", [128, KC])
    n2g_d = din("n2g", [128, KC])
    qg_d = din("qg", [128, QRANK // 128])
    kvg_d = din("kvg", [128, KVRANK // 128])
    fg_d = din("fg", [1, D])
    win_d = din("w_in", [D, 1600 + 9216 + 2 * D])
    wkrp_d = din("w_krp", [D, 64])
    wuq_d = din("w_uq", [QRANK, MLA_H * 256])
    wukv_d = din("w_ukv", [KVRANK, MLA_H * 256])
    wpa_d = din("w_pa", [MLA_H * 128, D])
    wpb_d = din("w_pb", [DIL_H * 128, D])
    wout_d = din("w_out", [D, D])
    wg_d = din("w_gate", [D, DFF])
    wu_d = din("w_up", [D, DFF])
    wd_d = din("w_down", [DFF, D])
    ident_d = din("ident", [128, 128], BF16)
    masks_d = din("masks", [128, NSEQ * 128], BF16)
    rope_d = din("ropec", [128, 4])
    out_d = nc.dram_tensor("out", [NOWN, D], F32, kind="ExternalOutput").ap()

    modrow = dscr("modrow", [1, 6 * D], F32)
    KR = dscr("KR", [64, SEQ])
    QN = dscr("QN", [MLA_H, 128, NOWN])
    QR = dscr("QR", [MLA_H, 64, NOWN])
    KN = dscr("KN", [MLA_H, 128, SEQ])
    VA = dscr("VA", [MLA_H, 128, SEQ // 128, 129])
    NKD = NOWN + NHALO
    QD = dscr("QD", [24, 128, NOWN])
    KD = dscr("KD", [24, 128, NKD])
    VD = dscr("VD", [24, 128, NKD // 128, 129])
    GT = dscr("GT", [2 * NMC, 128, NOWN])
    QD2 = dscr("QD2", [24, 128, NOWN])
    KD2 = dscr("KD2", [24, 128, NKD])
    YT = dscr("YT", [24, 128, NOWN])
    X1 = dscr("X1", [NOWN, D], F32)

    C_Q, C_KV, C_KR, C_DQ = 0, 1024, 1536, 1600
    C_DK, C_DV, C_G = 1600 + 3072, 1600 + 6144, 1600 + 9216

    with ExitStack() as es:
        P = Prog(nc, es)
        ident = P.sb(es, [128, 128], BF16, "ident")
        ones_bf = P.sb(es, [128, 128], BF16, "ones")
        ones_f = P.sb(es, [1, 128], F32, "onesf")
        modT = P.sb(es, [128, 6 * KC], F32, "modT")
        G1 = P.sb(es, [128, KC], F32, "G1")
        G2 = P.sb(es, [128, KC], F32, "G2")
        n1g = P.sb(es, [128, KC], F32, "n1g")
        n2g = P.sb(es, [128, KC], F32, "n2g")
        qg = P.sb(es, [128, 8], F32, "qg")
        kvg = P.sb(es, [128, 4], F32, "kvg")
        ropec = P.sb(es, [128, 4], F32, "ropec")
        consts = [ident, ones_bf, ones_f, modT, G1, G2, n1g, n2g, qg, kvg, ropec]
        Wg = WStream(P, es)

        P.dma("sp", ident.t[:], ident_d, writes=[ident])
        P.dma("sp", n1g.t[:], n1g_d, writes=[n1g])
        P.dma("sp", n2g.t[:], n2g_d, writes=[n2g])
        P.dma("sp", qg.t[:], qg_d, writes=[qg])
        P.dma("sp", kvg.t[:], kvg_d, writes=[kvg])
        P.dma("sp", ropec.t[:], rope_d, writes=[ropec])
        P.op("dve", lambda e: e.memset(ones_bf.t[:], 1.0), writes=[ones_bf])
        P.op("dve", lambda e: e.memset(ones_f.t[:], 1.0), writes=[ones_f])

        if stop == -1:
            P.barrier()
            P.dead = True

        def chk(n):
            if stop == n:
                P.barrier()
                P.dead = True

        try:
            if stop == -2:
                P.barrier()
                raise _Stop()
            with ExitStack() as ph:
                W = Wg
                cf = P.sb(ph, [128, KC], F32, "cf")
                cact = P.sb(ph, [128, KC], BF16, "cact")
                rowr = Ring([P.sb(ph, [1, 512], F32, "row") for _ in range(2)])
                brr = Ring([P.sb(ph, [1, 512], F32, "brow") for _ in range(2)])
                pmr = Ring([P.ps(ph, [128, 512], F32, "pm") for _ in range(2)])
                ptr = Ring([P.ps(ph, [128, 512], F32, "pt") for _ in range(2)])
                P.dma("sp", cf.t[:], c_d, writes=[cf])
                P.op("act", lambda e: e.activation(out=cact.t[:], in_=cf.t[:], func=AF.Silu),
                     reads=[cf], writes=[cact])
                for cb in range(6 * D // 512):
                    pm = pmr.next()
                    for (k0, kb) in kgroups(KC):
                        wb, wv = W.get(wada_d[k0 * 128:(k0 + kb) * 128, cb * 512:(cb + 1) * 512], kb, 512)

                        def f(e, k0=k0, kb=kb, wv=wv, pm=pm):
                            for k in range(kb):
                                ins = e.matmul(pm.t[0:1, :], lhsT=cact.t[:, k0 + k:k0 + k + 1], rhs=wv[:, k, :],
                                               start=(k0 + k == 0), stop=(k0 + k == KC - 1))
                            return ins
                        P.op("pe", f, reads=[wb, cact], writes=[pm])
                    br = brr.next()
                    row = rowr.next()
                    P.dma("sp", br.t[:], bada_d[0:1, cb * 512:(cb + 1) * 512], writes=[br])
                    P.op("dve", lambda e, row=row, pm=pm, br=br: e.tensor_tensor(
                        out=row.t[:], in0=pm.t[0:1, :], in1=br.t[:], op=ALU.add), reads=[pm, br], writes=[row])
                    P.dma("sp", modrow[0:1, cb * 512:(cb + 1) * 512], row.t[:], reads=[row])
                    pt = ptr.next()

                    def f2(e, row=row, pt=pt):
                        for j in range(4):
                            ins = e.matmul(pt.t[:, j:j + 1], lhsT=row.t[0:1, j * 128:(j + 1) * 128],
                                           rhs=ones_f.t[0:1, 0:1], start=True, stop=True)
                        return ins
                    P.op("pe", f2, reads=[row, ones_f], writes=[pt])
                    P.op("dve", lambda e, pt=pt, cb=cb: e.tensor_copy(out=modT.t[:, cb * 4:cb * 4 + 4], in_=pt.t[:, 0:4]),
                         reads=[pt], writes=[modT])
                P.op("dve", lambda e: e.scalar_tensor_tensor(out=G1.t[:], in0=modT.t[:, KC:2 * KC], scalar=1.0,
                                                             in1=n1g.t[:], op0=ALU.add, op1=ALU.mult),
                     reads=[modT, n1g], writes=[G1])
                P.op("dve", lambda e: e.scalar_tensor_tensor(out=G2.t[:], in0=modT.t[:, 4 * KC:5 * KC], scalar=1.0,
                                                             in1=n2g.t[:], op0=ALU.add, op1=ALU.mult),
                     reads=[modT, n2g], writes=[G2])
                P.barrier()
                chk(1)

            def rsqrt_ops(src_ap, src_bufs, dst_ap, dst_buf, scale):
                P.op("act", lambda e: e.activation(out=dst_ap, in_=src_ap, func=AF.Sqrt, scale=scale, bias=EPS),
                     reads=src_bufs, writes=[dst_buf])
                P.op("dve", lambda e: e.reciprocal(out=dst_ap, in_=dst_ap), reads=[dst_buf], writes=[dst_buf])

            def rope_table(posf_ap, posf_buf, inv_ap, phase, scale, dst_ap, dst_buf, tA, tA_ap, tI, tI_ap, tB, tB_ap):
                P.op("dve", lambda e: e.tensor_scalar(out=tA_ap, in0=posf_ap, scalar1=inv_ap, scalar2=phase,
                                                      op0=ALU.mult, op1=ALU.add), reads=[posf_buf, ropec], writes=[tA])
                P.op("dve", lambda e: e.tensor_copy(out=tI_ap, in_=tA_ap), reads=[tA], writes=[tI])
                P.op("dve", lambda e: e.tensor_copy(out=tB_ap, in_=tI_ap), reads=[tI], writes=[tB])
                P.op("dve", lambda e: e.tensor_tensor(out=tB_ap, in0=tA_ap, in1=tB_ap, op=ALU.subtract),
                     reads=[tA, tB], writes=[tB])
                P.op("dve", lambda e: e.scalar_tensor_tensor(out=tA_ap, in0=tB_ap, scalar=0.5, in1=tB_ap,
                                                             op0=ALU.is_gt, op1=ALU.subtract),
                     reads=[tB], writes=[tA])
                P.op("act", lambda e: e.activation(out=dst_ap, in_=tA_ap, func=AF.Sin, scale=scale),
                     reads=[tA, ropec], writes=[dst_buf])

            def build_hT(ph, src_rows, Gt, sh_col0, hT, nbuf=2, nh=1):
                Dh = D // nh
                KCh = KC // nh
                with ExitStack() as sc:
                    xr = Ring([P.sb(sc, [128, D], F32, "xt") for _ in range(nbuf)])
                    xsr = Ring([P.sb(sc, [128, Dh], BF16, "xs") for _ in range(nbuf)])
                    ssr = Ring([P.sb(sc, [128, 4], F32, "ss") for _ in range(2)])
                    ptr = Ring([P.ps(sc, [128, 8, 128], BF16, "ptT") for _ in range(3)])
                    for tt in range(TS // 128):
                        xt = xr.next()
                        xs = xsr.next()
                        ss = ssr.next()
                        P.dma("sp", xt.t[:], src_rows(tt), writes=[xt])
                        P.op("dve", lambda e, ss=ss: e.memset(ss.t[:], 0.0), writes=[ss])
                        for hh in range(nh):
                            P.op("act", lambda e, xt=xt, xs=xs, ss=ss, hh=hh: e.activation(
                                out=xs.t[:], in_=xt.t[:, hh * Dh:(hh + 1) * Dh], func=AF.Square,
                                accum_out=ss.t[:, hh:hh + 1]), reads=[xt], writes=[xs, ss])
                        if nh == 2:
                            P.op("dve", lambda e, ss=ss: e.tensor_tensor(out=ss.t[:, 2:3], in0=ss.t[:, 0:1],
                                                                         in1=ss.t[:, 1:2], op=ALU.add),
                                 reads=[ss], writes=[ss])
                            tot = ss.t[:, 2:3]
                        else:
                            tot = ss.t[:, 0:1]
                        rsqrt_ops(tot, [ss], ss.t[:, 3:4], ss, 1.0 / D)
                        for hh in range(nh):
                            P.op("act", lambda e, xt=xt, xs=xs, ss=ss, hh=hh: e.activation(
                                out=xs.t[:], in_=xt.t[:, hh * Dh:(hh + 1) * Dh], func=AF.Identity,
                                scale=ss.t[:, 3:4]), reads=[xt, ss], writes=[xs])
                            for k0 in range(0, KCh, 8):
                                kn = min(8, KCh - k0)
                                pt = ptr.next()

                                def f(e, k0=k0, kn=kn, pt=pt, xs=xs):
                                    for j in range(kn):
                                        ins = e.transpose(pt.t[:, j, :], xs.t[:, (k0 + j) * 128:(k0 + j + 1) * 128],
                                                          ident.t[:])
                                    return ins
                                P.op("pe", f, reads=[xs, ident], writes=[pt])
                                for j in range(kn):
                                    kc = hh * KCh + k0 + j
                                    if kc % 2 == 0:
                                        P.op("act", lambda e, pt=pt, j=j, kc=kc, tt=tt: e.activation(
                                            out=hT.t[:, kc, tt * 128:(tt + 1) * 128], in_=pt.t[:, j, :],
                                            func=AF.Identity, scale=Gt.t[:, kc:kc + 1],
                                            bias=modT.t[:, sh_col0 + kc:sh_col0 + kc + 1]),
                                            reads=[pt, Gt, modT], writes=[hT])
                                    else:
                                        P.op("dve", lambda e, pt=pt, j=j, kc=kc, tt=tt: e.tensor_scalar(
                                            out=hT.t[:, kc, tt * 128:(tt + 1) * 128], in0=pt.t[:, j, :],
                                            scalar1=Gt.t[:, kc:kc + 1],
                                            scalar2=modT.t[:, sh_col0 + kc:sh_col0 + kc + 1],
                                            op0=ALU.mult, op1=ALU.add), reads=[pt, Gt, modT], writes=[hT])
                    P.barrier()

            def gemm_fm(W, wap, kch, act_of, ncols, M, psr, epi, act_bufs, kp=128):
                cbmax = max(M, min(512, (WStream.SLOT // kch) // M * M))
                for c0 in range(0, ncols, cbmax):
                    cw = min(cbmax, ncols - c0)
                    wb, wv = W.get(wap[:, c0:c0 + cw], kch, cw, kp)
                    for m0 in range(0, cw, M):
                        ps = psr.next()

                        def f(e, wv=wv, m0=m0, ps=ps):
                            for k in range(kch):
                                ins = e.matmul(ps.t[0:M, :], lhsT=wv[:, k, m0:m0 + M], rhs=act_of(k),
                                               start=(k == 0), stop=(k == kch - 1))
                            return ins
                        P.op("pe", f, reads=[wb] + act_bufs, writes=[ps])
                        epi((c0 + m0) // M, ps)

            def gemm_tm(W, wap, kch, actT_of, ncols, psets, epi, act_bufs):
                kg = kgroups(kch)
                for c0 in range(0, ncols, 512):
                    cw = min(512, ncols - c0)
                    pset = psets.next()
                    for (k0, kb) in kg:
                        wb, wv = W.get(wap[k0 * 128:(k0 + kb) * 128, c0:c0 + cw], kb, cw)

                        def f(e, wv=wv, k0=k0, kb=kb, pset=pset, cw=cw):
                            for tt in range(4):
                                for k in range(kb):
                                    ins = e.matmul(pset[tt].t[:, 0:cw], lhsT=actT_of(k0 + k, tt), rhs=wv[:, k, :],
                                                   start=(k0 + k == 0), stop=(k0 + k == kch - 1))
                            return ins
                        P.op("pe", f, reads=[wb] + act_bufs, writes=list(pset))
                    for tt in range(4):
                        epi(c0, cw, tt, pset[tt])

            slabs = [("own", s * TS) for s in range(NOWN // TS)] + \
                    [("halo", NOWN + s * TS) for s in range(NHALO // TS)] + \
                    [("far", NOWN + NHALO + s * TS) for s in range((SEQ - NOWN - NHALO) // TS)]
            for (kind, t0) in slabs:
                with ExitStack() as ph:
                    hT = P.sb(ph, [128, KC, TS], BF16, "hT")
                    build_hT(ph, lambda tt, t0=t0: x_d[t0 + tt * 128:t0 + (tt + 1) * 128, :], G1, 0, hT)
                    chk(21)
                    W = Wg
                    psr = Ring([P.ps(ph, [128, 512], F32, "psA") for _ in range(2)])
                    pss = P.ps(ph, [128, 512], F32, "pss")
                    psc = P.ps(ph, [128, 512], F32, "psc")
                    psets = Ring([[P.ps(ph, [128, 512], F32, "pst") for _ in range(4)]])
                    zqn = P.sb(ph, [128, 8, TS], BF16, "zqn")
                    zkvn = P.sb(ph, [128, 4, TS], BF16, "zkvn")
                    sqr = Ring([P.sb(ph, [128, TS], BF16, "sq") for _ in range(5)])
                    rq = P.sb(ph, [128, TS], F32, "rq")
                    rkv = P.sb(ph, [128, TS], F32, "rkv")
                    rkvc = P.sb(ph, [128, 8], F32, "rkvc")
                    stg = Ring([P.sb(ph, [128, TS], BF16, "stg") for _ in range(3)])
                    stv = Ring([P.sb(ph, [128, 4, 129], BF16, "stv") for _ in range(2)])
                    tmpf = Ring([P.sb(ph, [128, TS], F32, "tmpf") for _ in range(3)])
                    posi = P.sb(ph, [64, TS], I32, "posi")
                    posf = P.sb(ph, [64, TS], F32, "posf")
                    cosr = P.sb(ph, [64, TS], F32, "cosr")
                    sinr = P.sb(ph, [64, TS], F32, "sinr")
                    for b in stv.bufs:
                        P.op("dve", lambda e, b=b: e.memset(b.t[:], 1.0), writes=[b])
                    P.dma("sp", posi.t[:], pos_d[0:1, t0:t0 + TS].partition_broadcast(64), writes=[posi])
                    P.op("dve", lambda e: e.tensor_copy(out=posf.t[:], in_=posi.t[:]), reads=[posi], writes=[posf])
                    tA1, tB1 = tmpf.next(), tmpf.next()
                    rope_table(posf.t[:], posf, ropec.t[0:64, 2:3], 0.25, -2 * PI, cosr.t[:], cosr,
                               tA1, tA1.t[0:64, :], posi, posi.t[:], tB1, tB1.t[0:64, :])
                    rope_table(posf.t[:], posf, ropec.t[0:64, 2:3], 0.0, ropec.t[0:64, 3:4], sinr.t[:], sinr,
                               tA1, tA1.t[0:64, :], posi, posi.t[:], tB1, tB1.t[0:64, :])

                    act_h = lambda k: hT.t[:, k, :]
                    chk(22)

                    def latent(col0, nch, gt, zn, rbc, want_col):
                        sqs = []

                        def epi(blk, ps):
                            import os
                            LAT = int(os.environ.get("LAT", "9"))
                            if LAT < 2:
                                return
                            sq = sqr.next()
                            P.op("act", lambda e: e.activation(out=sq.t[:], in_=ps.t[:], func=AF.Square),
                                 reads=[ps], writes=[sq])
                            if LAT < 3:
                                return
                            P.op("dve", lambda e: e.tensor_scalar(
                                out=zn.t[:, blk, :], in0=ps.t[:], scalar1=gt.t[:, blk:blk + 1], scalar2=0.0,
                                op0=ALU.mult, op1=ALU.add), reads=[ps, gt], writes=[zn])

                            def f(e):
                                return e.matmul(pss.t[:], lhsT=ones_bf.t[:], rhs=sq.t[:], start=(blk == 0),
                                                stop=(blk == nch - 1))
                            P.op("pe", f, reads=[sq, ones_bf], writes=[pss])
                            sqs.append(sq)
                        gemm_fm(W, win_d[:, col0:col0 + nch * 128], KC, act_h, nch * 128, 128, psr, epi, [hT])
                        rsqrt_ops(pss.t[:], [pss], rbc.t[:], rbc, 1.0 / (nch * 128))
                        if want_col:
                            def fc(e):
                                for tt in range(4):
                                    for bi, sq in enumerate(sqs):
                                        ins = e.matmul(psc.t[:, tt:tt + 1], lhsT=sq.t[:, tt * 128:(tt + 1) * 128],
                                                       rhs=ones_bf.t[:, 0:1], start=(bi == 0), stop=(bi == nch - 1))
                                return ins
                            P.op("pe", fc, reads=list(sqs) + [ones_bf], writes=[psc])
                            rsqrt_ops(psc.t[:, 0:4], [psc], rkvc.t[:, 0:4], rkvc, 1.0 / (nch * 128))

                    def rope64(ps_raw, ps_perm, dstT, extra=None):
                        t1 = tmpf.next()
                        t2 = tmpf.next()
                        P.op("dve", lambda e: e.tensor_tensor(out=t1.t[0:64, :], in0=ps_raw.t[0:64, :], in1=cosr.t[:],
                                                              op=ALU.mult), reads=[ps_raw, cosr], writes=[t1])
                        P.op("dve", lambda e: e.tensor_tensor(out=t2.t[0:64, :], in0=ps_perm.t[0:64, :], in1=sinr.t[:],
                                                              op=ALU.mult), reads=[ps_perm, sinr], writes=[t2])
                        if extra is None:
                            P.op("dve", lambda e: e.tensor_tensor(out=dstT.t[0:64, :], in0=t1.t[0:64, :], in1=t2.t[0:64, :],
                                                                  op=ALU.add), reads=[t1, t2], writes=[dstT])
                        else:
                            P.op("dve", lambda e: e.tensor_tensor(out=t1.t[0:64, :], in0=t1.t[0:64, :], in1=t2.t[0:64, :],
                                                                  op=ALU.add), reads=[t1, t2], writes=[t1])
                            P.op("dve", lambda e: e.tensor_tensor(out=dstT.t[0:64, :], in0=t1.t[0:64, :],
                                                                  in1=extra.t[0:64, :], op=ALU.mult),
                                 reads=[t1, extra], writes=[dstT])

                    def store_fm(dst_ap, st, rows=128):
                        P.dma("sp", dst_ap, st.t[0:rows, :], reads=[st])

                    if kind == "own":
                        latent(C_Q, 8, qg, zqn, rq, False)
                        chk(23)
                    latent(C_KV, 4, kvg, zkvn, rkv, True)
                    chk(24)

                    hold = {}

                    def epi_kr_raw(blk, ps):
                        hold["raw"] = ps
                    gemm_fm(W, win_d[:, C_KR:C_KR + 64], KC, act_h, 64, 64, psr, epi_kr_raw, [hT])

                    def epi_kr_perm(blk, ps):
                        st = stg.next()
                        rope64(hold["raw"], ps, st)
                        store_fm(KR[:, t0:t0 + TS], st, 64)
                    gemm_fm(W, wkrp_d, KC, act_h, 64, 64, psr, epi_kr_perm, [hT])
                    chk(25)

                    def plain_store(dst_of, func=None):
                        cnt = [0]

                        def epi(blk, ps):
                            st = stg.next()
                            if func is not None:
                                P.op("act", lambda e: e.activation(out=st.t[:], in_=ps.t[:], func=func),
                                     reads=[ps], writes=[st])
                            elif cnt[0] % 2 == 0:
                                P.op("act", lambda e: e.activation(out=st.t[:], in_=ps.t[:], func=AF.Identity),
                                     reads=[ps], writes=[st])
                            else:
                                P.op("dve", lambda e: e.tensor_copy(out=st.t[:], in_=ps.t[:]), reads=[ps], writes=[st])
                            cnt[0] += 1
                            store_fm(dst_of(blk), st)
                        return epi

                    def v_epi(dst, h_base, ctile0, scale_col=None):
                        def epi(c0, cw, tt, ps):
                            st = stv.next()
                            src = ps.t[:, 0:cw].rearrange("p (h e) -> p h e", e=128)
                            nh = cw // 128
                            if scale_col is None:
                                P.op("act", lambda e: e.activation(out=st.t[:, 0:nh, 0:128], in_=src, func=AF.Identity),
                                     reads=[ps], writes=[st])
                            else:
                                P.op("act", lambda e: e.activation(out=st.t[:, 0:nh, 0:128], in_=src, func=AF.Identity,
                                                                   scale=scale_col.t[:, tt:tt + 1]),
                                     reads=[ps, scale_col], writes=[st])
                            h0 = h_base + c0 // 128
                            P.dma("sp", dst[h0:h0 + nh, :, ctile0 + tt, :].rearrange("h p e -> p h e"),
                                  st.t[:, 0:nh, :], reads=[st])
                        return epi

                    actT_h = lambda k, tt: hT.t[:, k, tt * 128:(tt + 1) * 128]
                    chk(26)
                    if kind == "own":
                        gemm_fm(W, win_d[:, C_DQ:C_DQ + 3072], KC, act_h, 3072, 128, psr,
                                plain_store(lambda blk: QD[blk, :, t0:t0 + TS]), [hT])
                    if kind in ("own", "halo"):
                        gemm_fm(W, win_d[:, C_DK:C_DK + 3072], KC, act_h, 3072, 128, psr,
                                plain_store(lambda blk: KD[blk, :, t0:t0 + TS]), [hT])
                        gemm_tm(W, win_d[:, C_DV:C_DV + 3072], KC, actT_h, 3072, psets,
                                v_epi(VD, 0, t0 // 128), [hT])
                    if kind == "own":
                        gemm_fm(W, win_d[:, C_G:C_G + 2 * D], KC, act_h, 2 * D, 128, psr,
                                plain_store(lambda blk: GT[blk, :, t0:t0 + TS], AF.Sigmoid), [hT])

                        act_q = lambda k: zqn.t[:, k, :]
                        for h in range(MLA_H):
                            qh = {}

                            def epi_n(blk, ps, h=h):
                                st = stg.next()
                                P.op("dve", lambda e: e.tensor_tensor(out=st.t[:], in0=ps.t[:], in1=rq.t[:], op=ALU.mult),
                                     reads=[ps, rq], writes=[st])
                                store_fm(QN[h, :, t0:t0 + TS], st)
                            gemm_fm(W, wuq_d[:, h * 256:h * 256 + 128], 8, act_q, 128, 128, psr, epi_n, [zqn])

                            def epi_r(blk, ps, h=h, qh=qh):
                                if blk == 0:
                                    qh["raw"] = ps
                                else:
                                    st = stg.next()
                                    rope64(qh["raw"], ps, st, extra=rq)
                                    store_fm(QR[h, :, t0:t0 + TS], st, 64)
                            gemm_fm(W, wuq_d[:, h * 256 + 128:h * 256 + 256], 8, act_q, 128, 64, psr, epi_r, [zqn])

                    act_kv = lambda k: zkvn.t[:, k, :]

                    def epi_kn(blk, ps):
                        st = stg.next()
                        P.op("dve", lambda e: e.tensor_tensor(out=st.t[:], in0=ps.t[:], in1=rkv.t[:], op=ALU.mult),
                             reads=[ps, rkv], writes=[st])
                        store_fm(KN[blk, :, t0:t0 + TS], st)
                    gemm_fm(W, wukv_d[:, 0:MLA_H * 128], 4, act_kv, MLA_H * 128, 128, psr, epi_kn, [zkvn])
                    actT_kv = lambda k, tt: zkvn.t[:, k, tt * 128:(tt + 1) * 128]
                    gemm_tm(W, wukv_d[:, MLA_H * 128:MLA_H * 256], 4, actT_kv, MLA_H * 128, psets,
                            v_epi(VA, 0, t0 // 128, scale_col=rkvc), [zkvn])
                    P.barrier()
                    chk(2)

            def attn_epilogue(O, qs_list, yrow, ytile_r, ptT_r, yst, q0, rden_r):
                for i, (Ob, Ov) in enumerate(qs_list):
                    rd = rden_r.next()
                    yt = ytile_r.next()
                    P.op("dve", lambda e, Ov=Ov, rd=rd: e.reciprocal(out=rd.t[:], in_=Ov[:, 128:129]),
                         reads=[Ob], writes=[rd])
                    P.op("act", lambda e, Ov=Ov, rd=rd, yt=yt: e.activation(
                        out=yt.t[:], in_=Ov[:, 0:128], func=AF.Identity, scale=rd.t[:, 0:1]),
                        reads=[Ob, rd], writes=[yt])
                    pt = ptT_r.next()
                    P.op("pe", lambda e, pt=pt, yt=yt: e.transpose(pt.t[:, 0:128], yt.t[:], ident.t[:]),
                         reads=[yt, ident], writes=[pt])
                    P.op("dve", lambda e, pt=pt, i=i: e.tensor_copy(out=yst.t[:, i * 128:(i + 1) * 128], in_=pt.t[:, 0:128]),
                         reads=[pt], writes=[yst])
                n = len(qs_list)
                P.dma("sp", YT[yrow, :, q0:q0 + n * 128], yst.t[:, 0:n * 128], reads=[yst])

            sc_mla = (128 + 64) ** -0.5
            with ExitStack() as ph:
                krT = P.sb(ph, [64, SEQ], BF16, "krT")
                P.dma("sp", krT.t[:], KR, writes=[krT])
                hsets = Ring([dict(qn=P.sb(ph, [128, NOWN], BF16, "qn"), qr=P.sb(ph, [64, NOWN], BF16, "qr"),
                                   kn=P.sb(ph, [128, SEQ], BF16, "kn"), va=P.sb(ph, [128, SEQ // 128, 129], BF16, "va"))
                              for _ in range(2)])
                pS = Ring([P.ps(ph, [128, 512], F32, "pS") for _ in range(3)])
                pO = Ring([[P.ps(ph, [128, 512], F32, "pOn"), P.ps(ph, [128, 512], F32, "pOd")] for _ in range(2)])
                Pr = Ring([P.sb(ph, [128, 512], BF16, "P") for _ in range(4)])
                ystr = Ring([P.sb(ph, [128, 512], BF16, "yst") for _ in range(2)])
                rdr = Ring([P.sb(ph, [128, 512], F32, "rd") for _ in range(2)])
                NKC = SEQ // 128
                cosF = P.sb(ph, [128, NKD], F32, "cosF")
                sinF = P.sb(ph, [128, NKD], F32, "sinF")
                pi_ = P.sb(ph, [128, 512], I32, "pi")
                pf_ = P.sb(ph, [128, 512], F32, "pf")
                ta_ = P.sb(ph, [128, 512], F32, "ta")
                tb_ = P.sb(ph, [128, 512], F32, "tb")
                ti_ = P.sb(ph, [128, 512], I32, "ti")
                for c0 in range(0, NKD, 512):
                    P.dma("sp", pi_.t[:], pos_d[0:1, c0:c0 + 512].partition_broadcast(128), writes=[pi_])
                    P.op("dve", lambda e: e.tensor_copy(out=pf_.t[:], in_=pi_.t[:]), reads=[pi_], writes=[pf_])
                    rope_table(pf_.t[:], pf_, ropec.t[:, 0:1], 0.25, -2 * PI, cosF.t[:, c0:c0 + 512], cosF,
                               ta_, ta_.t[:], ti_, ti_.t[:], tb_, tb_.t[:])
                    rope_table(pf_.t[:], pf_, ropec.t[:, 0:1], 0.0, ropec.t[:, 1:2], sinF.t[:, c0:c0 + 512], sinF,
                               ta_, ta_.t[:], ti_, ti_.t[:], tb_, tb_.t[:])
                rawr = Ring([P.sb(ph, [128, NKD], BF16, "raw") for _ in range(1)])
                swpr = Ring([P.sb(ph, [128, NKD], BF16, "swp") for _ in range(1)])
                ror = Ring([P.sb(ph, [128, NKD], BF16, "ro") for _ in range(1)])
                RC = 1536
                t1r = Ring([P.sb(ph, [128, RC], F32, "t1") for _ in range(1)])
                t2r = Ring([P.sb(ph, [128, RC], F32, "t2") for _ in range(1)])

                def rope_job(src, n, dst):
                    raw = rawr.next()
                    swp = swpr.next()
                    ro = ror.next()
                    P.dma("sp", raw.t[:, 0:n], src, writes=[raw])
                    P.dma("sp", swp.t[0:64, 0:n], src[64:128, :], writes=[swp])
                    P.dma("sp", swp.t[64:128, 0:n], src[0:64, :], writes=[swp])
                    for c0 in range(0, n, RC):
                        cw = min(RC, n - c0)
                        t1 = t1r.next()
                        t2 = t2r.next()
                        P.op("dve", lambda e, t1=t1, c0=c0, cw=cw: e.tensor_tensor(
                            out=t1.t[:, 0:cw], in0=raw.t[:, c0:c0 + cw], in1=cosF.t[:, c0:c0 + cw], op=ALU.mult),
                            reads=[raw, cosF], writes=[t1])
                        P.op("dve", lambda e, t2=t2, c0=c0, cw=cw: e.tensor_tensor(
                            out=t2.t[:, 0:cw], in0=swp.t[:, c0:c0 + cw], in1=sinF.t[:, c0:c0 + cw], op=ALU.mult),
                            reads=[swp, sinF], writes=[t2])
                        P.op("dve", lambda e, t1=t1, t2=t2, c0=c0, cw=cw: e.tensor_tensor(
                            out=ro.t[:, c0:c0 + cw], in0=t1.t[:, 0:cw], in1=t2.t[:, 0:cw], op=ALU.add),
                            reads=[t1, t2], writes=[ro])
                    P.dma("sp", dst, ro.t[:, 0:n], reads=[ro])
                rope_jobs = []
                for i in range(24):
                    rope_jobs.append((QD[i], NOWN, QD2[i]))
                    rope_jobs.append((KD[i], NKD, KD2[i]))

                def load_head(h):
                    s = hsets.next()
                    P.dma("sp", s["qn"].t[:], QN[h], writes=[s["qn"]])
                    P.dma("sp", s["qr"].t[:], QR[h], writes=[s["qr"]])
                    P.dma("sp", s["kn"].t[:], KN[h], writes=[s["kn"]])
                    P.dma("sp", s["va"].t[:], VA[h], writes=[s["va"]])
                    return s
                nxt = load_head(0)
                for h in range(MLA_H):
                    s = nxt
                    if h + 1 < MLA_H:
                        nxt = load_head(h + 1)
                    for _ in range(3):
                        if rope_jobs:
                            rope_job(*rope_jobs.pop(0))
                    for qg_ in range(NOWN // 512):
                        On, Od = pO.next()

                        def emitS(kc, s=s, qg_=qg_):
                            ps = pS.next()

                            def f(e):
                                e.matmul(ps.t[:], lhsT=s["kn"].t[:, kc * 128:(kc + 1) * 128],
                                         rhs=s["qn"].t[:, qg_ * 512:(qg_ + 1) * 512], start=True, stop=False)
                                return e.matmul(ps.t[:], lhsT=krT.t[:, kc * 128:(kc + 1) * 128],
                                                rhs=s["qr"].t[:, qg_ * 512:(qg_ + 1) * 512], start=False, stop=True)
                            P.op("pe", f, reads=[s["kn"], s["qn"], s["qr"], krT], writes=[ps])
                            return ps
                        ps_next = emitS(0)
                        for kc in range(NKC):
                            ps = ps_next
                            if kc + 1 < NKC:
                                ps_next = emitS(kc + 1)
                            pb = Pr.next()
                            P.op("act", lambda e, ps=ps, pb=pb: e.activation(out=pb.t[:], in_=ps.t[:], func=AF.Exp,
                                                                             scale=sc_mla), reads=[ps], writes=[pb])

                            def f(e, pb=pb, kc=kc, On=On, Od=Od, s=s):
                                e.matmul(On.t[:], lhsT=s["va"].t[:, kc, 0:128], rhs=pb.t[:],
                                         start=(kc == 0), stop=(kc == NKC - 1))
                                return e.matmul(Od.t[:], lhsT=ones_bf.t[:], rhs=pb.t[:],
                                                start=(kc == 0), stop=(kc == NKC - 1))
                            P.op("pe", f, reads=[pb, s["va"], ones_bf], writes=[On, Od])
                        rd = rdr.next()
                        yst = ystr.next()
                        P.op("dve", lambda e, Od=Od, rd=rd: e.reciprocal(out=rd.t[:], in_=Od.t[:]),
                             reads=[Od], writes=[rd])
                        P.op("dve", lambda e, On=On, rd=rd, yst=yst: e.tensor_tensor(
                            out=yst.t[:], in0=On.t[:], in1=rd.t[:], op=ALU.mult), reads=[On, rd], writes=[yst])
                        P.dma("sp", YT[h, :, qg_ * 512:(qg_ + 1) * 512], yst.t[:], reads=[yst])
                P.barrier()
                chk(4)

            sc_dil = 128 ** -0.5
            NT3 = NKD // 128 + 8
            with ExitStack() as ph:
                masks = P.sb(ph, [128, NSEQ, 128], BF16, "masks")
                P.dma("sp", masks.t[:].rearrange("p s i -> p (s i)"), masks_d, writes=[masks])
                hs3 = Ring([dict(q=[P.sb(ph, [128, NOWN], BF16, "q3") for _ in range(3)],
                                 k=[P.sb(ph, [128, NT3 * 128], BF16, "k3") for _ in range(3)],
                                 v=[P.sb(ph, [128, NT3, 129], BF16, "v3") for _ in range(3)]) for _ in range(2)])
                for s in hs3.bufs:
                    for g in range(3):
                        P.op("dve", lambda e, b=s["k"][g]: e.memset(b.t[:, 0:1024], 0.0), writes=[s["k"][g]])
                        P.op("dve", lambda e, b=s["v"][g]: e.memset(b.t[:, 0:8, :], 0.0), writes=[s["v"][g]])
                pS = Ring([P.ps(ph, [128, 4, 128], F32, "pS3") for _ in range(3)])
                pO = Ring([P.ps(ph, [128, 512], F32, "pO3") for _ in range(2)])
                ptT = Ring([P.ps(ph, [128, 1024], BF16, "ptT3") for _ in range(1)])
                Pr = Ring([P.sb(ph, [128, 4, 128], BF16, "P3") for _ in range(3)])
                Pm = Ring([P.sb(ph, [128, 4, 128], BF16, "Pm3") for _ in range(3)])
                ytr = Ring([P.sb(ph, [128, 128], BF16, "yt3") for _ in range(2)])
                ystr = Ring([P.sb(ph, [128, 512], BF16, "yst3") for _ in range(2)])
                rdr = Ring([P.sb(ph, [128, 1], F32, "rd3") for _ in range(4)])

                def load_head3(h):
                    s = hs3.next()
                    for g in range(3):
                        P.dma("sp", s["q"][g].t[:], QD2[g * 8 + h], writes=[s["q"][g]])
                        P.dma("sp", s["k"][g].t[:, 1024:1024 + NKD], KD2[g * 8 + h], writes=[s["k"][g]])
                        P.dma("sp", s["v"][g].t[:, 8:NT3, :], VD[g * 8 + h], writes=[s["v"][g]])
                    return s
                groups = [(i, min(4, NSEQ - i)) for i in range(0, NSEQ, 4)]
                nxt = load_head3(0)
                for h in range(DIL_H):
                    s = nxt
                    if h + 1 < DIL_H:
                        nxt = load_head3(h + 1)
                    for qb in range(NOWN // 128):
                        O = pO.next()

                        def emitS(gi, s=s, qb=qb):
                            i0, n = groups[gi]
                            ps = pS.next()

                            def f(e):
                                for j in range(n):
                                    g, rel = DIL_SEQ[i0 + j]
                                    kt = qb + rel + 8
                                    ins = e.matmul(ps.t[:, j, :], lhsT=s["k"][g].t[:, kt * 128:(kt + 1) * 128],
                                                   rhs=s["q"][g].t[:, qb * 128:(qb + 1) * 128], start=True, stop=True)
                                return ins
                            P.op("pe", f, reads=s["k"] + s["q"], writes=[ps])
                            return ps
                        ps_next = emitS(0)
                        for gi, (i0, n) in enumerate(groups):
                            ps = ps_next
                            if gi + 1 < len(groups):
                                ps_next = emitS(gi + 1)
                            pb = Pr.next()
                            pm = Pm.next()
                            P.op("act", lambda e, ps=ps, pb=pb, n=n: e.activation(
                                out=pb.t[:, 0:n, :], in_=ps.t[:, 0:n, :], func=AF.Exp, scale=sc_dil),
                                reads=[ps], writes=[pb])
                            P.op("dve", lambda e, pb=pb, pm=pm, i0=i0, n=n: e.tensor_tensor(
                                out=pm.t[:, 0:n, :], in0=pb.t[:, 0:n, :], in1=masks.t[:, i0:i0 + n, :], op=ALU.mult),
                                reads=[pb, masks], writes=[pm])

                            def f(e, pm=pm, i0=i0, n=n, O=O, s=s, qb=qb):
                                for j in range(n):
                                    g, rel = DIL_SEQ[i0 + j]
                                    kt = qb + rel + 8
                                    ins = e.matmul(O.t[:, 0:129], lhsT=pm.t[:, j, :], rhs=s["v"][g].t[:, kt, :],
                                                   start=(i0 + j == 0), stop=(i0 + j == NSEQ - 1))
                                return ins
                            P.op("pe", f, reads=[pm] + s["v"], writes=[O])
                        attn_epilogue(O, [(O, O.t[:, :])], 16 + h, ytr, ptT, ystr.next(), qb * 128, rdr)
                P.barrier()
                chk(5)

            for sidx in range(NOWN // TS):
                t0 = sidx * TS
                with ExitStack() as ph:
                    W = Wg
                    yT = P.sb(ph, [128, 24, TS], BF16, "yT")
                    mixT = P.sb(ph, [128, KC, TS], BF16, "mixT")
                    P.dma("sp", yT.t[:], YT[:, :, t0:t0 + TS].rearrange("c p t -> p c t"), writes=[yT])
                    psA = Ring([P.ps(ph, [128, 512], F32, "psA4") for _ in range(2)])
                    psB = Ring([P.ps(ph, [128, 512], F32, "psB4") for _ in range(2)])
                    psets = Ring([[P.ps(ph, [128, 512], F32, "pst4") for _ in range(4)]])
                    gar = Ring([P.sb(ph, [128, TS], BF16, "ga") for _ in range(2)])
                    gbr = Ring([P.sb(ph, [128, TS], BF16, "gb") for _ in range(2)])
                    tmr = Ring([P.sb(ph, [128, TS], F32, "tm4") for _ in range(2)])
                    xr = Ring([P.sb(ph, [128, 512], F32, "x4") for _ in range(3)])
                    gcr = Ring([P.sb(ph, [128, 512], F32, "g1bc") for _ in range(2)])
                    holdA = {}

                    def epiA(blk, ps):
                        holdA[blk] = ps
                        ga = gar.next()
                        P.dma("sp", ga.t[:], GT[blk, :, t0:t0 + TS], writes=[ga])
                        holdA[("ga", blk)] = ga

                    def epiB(blk, ps):
                        gb = gbr.next()
                        P.dma("sp", gb.t[:], GT[NMC + blk, :, t0:t0 + TS], writes=[gb])
                        pa = holdA.pop(blk)
                        ga = holdA.pop(("ga", blk))
                        tm = tmr.next()
                        P.op("dve", lambda e: e.tensor_tensor(out=tm.t[:], in0=pa.t[:], in1=ga.t[:], op=ALU.mult),
                             reads=[pa, ga], writes=[tm])
                        tm2 = tmr.next()
                        P.op("dve", lambda e: e.tensor_tensor(out=tm2.t[:], in0=ps.t[:], in1=gb.t[:], op=ALU.mult),
                             reads=[ps, gb], writes=[tm2])
                        P.op("dve", lambda e: e.tensor_tensor(out=mixT.t[:, blk, :], in0=tm.t[:], in1=tm2.t[:], op=ALU.add),
                             reads=[tm, tm2], writes=[mixT])
                    for mc in range(NMC):
                        gemm_fm(W, wpa_d[:, mc * 128:(mc + 1) * 128], 16, lambda k: yT.t[:, k, :], 128, 128, psA,
                                lambda blk, ps, mc=mc: epiA(mc, ps), [yT])
                        gemm_fm(W, wpb_d[:, mc * 128:(mc + 1) * 128], 8, lambda k: yT.t[:, 16 + k, :], 128, 128, psB,
                                lambda blk, ps, mc=mc: epiB(mc, ps), [yT])
                    gstate = {}

                    def epi_out(c0, cw, tt, ps):
                        if tt == 0:
                            gc = gcr.next()
                            P.dma("sp", gc.t[:, 0:cw], modrow[0:1, 2 * D + c0:2 * D + c0 + cw].partition_broadcast(128),
                                  writes=[gc])
                            gstate["gc"] = gc
                        gc = gstate["gc"]
                        xt = xr.next()
                        P.dma("sp", xt.t[:, 0:cw], x_d[t0 + tt * 128:t0 + (tt + 1) * 128, c0:c0 + cw], writes=[xt])
                        tm = tmr.next()
                        P.op("dve", lambda e: e.tensor_tensor(out=tm.t[:, 0:cw], in0=ps.t[:, 0:cw], in1=gc.t[:, 0:cw],
                                                              op=ALU.mult), reads=[ps, gc], writes=[tm])
                        P.op("dve", lambda e: e.tensor_tensor(out=xt.t[:, 0:cw], in0=tm.t[:, 0:cw], in1=xt.t[:, 0:cw],
                                                              op=ALU.add), reads=[tm, xt], writes=[xt])
                        P.dma("sp", X1[t0 + tt * 128:t0 + (tt + 1) * 128, c0:c0 + cw], xt.t[:, 0:cw], reads=[xt])
                    gemm_tm(W, wout_d, KC, lambda k, tt: mixT.t[:, k, tt * 128:(tt + 1) * 128], D, psets, epi_out, [mixT])
                    P.barrier()
                    chk(6)

                with ExitStack() as ph:
                    actT = P.sb(ph, [128, NFF, TS], BF16, "actT")
                    ssq = P.sb(ph, [128, 4, 8], F32, "ssq")
                    with ExitStack() as ph2:
                        h2T = P.sb(ph2, [128, KC, TS], BF16, "h2T")
                        build_hT(ph2, lambda tt, t0=t0: X1[t0 + tt * 128:t0 + (tt + 1) * 128, :], G2, 3 * KC, h2T, nbuf=1, nh=2)
                        W = Wg
                        psG = Ring([P.ps(ph2, [128, 512], F32, "psG") for _ in range(2)])
                        psU = Ring([P.ps(ph2, [128, 512], F32, "psU") for _ in range(2)])
                        sgr = Ring([P.sb(ph2, [128, TS], F32, "sg") for _ in range(5)])
                        holdG = {}

                        def epiG(blk, ps):
                            sg = sgr.next()
                            P.op("act", lambda e: e.activation(out=sg.t[:], in_=ps.t[:], func=AF.Silu),
                                 reads=[ps], writes=[sg])
                            holdG[blk] = sg

                        def epiU(blk, ps):
                            sg = holdG.pop(blk)
                            P.op("dve", lambda e: e.tensor_tensor(out=actT.t[:, blk, :], in0=ps.t[:], in1=sg.t[:],
                                                                  op=ALU.mult), reads=[ps, sg], writes=[actT])
                        fstep = max(128, min(512, (WStream.SLOT // KC) // 128 * 128))
                        for f0 in range(0, DFF, fstep):
                            fw = min(fstep, DFF - f0)
                            gemm_fm(W, wg_d[:, f0:f0 + fw], KC, lambda k: h2T.t[:, k, :], fw, 128, psG,
                                    lambda blk, ps, f0=f0: epiG(f0 // 128 + blk, ps), [h2T])
                            gemm_fm(W, wu_d[:, f0:f0 + fw], KC, lambda k: h2T.t[:, k, :], fw, 128, psU,
                                    lambda blk, ps, f0=f0: epiU(f0 // 128 + blk, ps), [h2T])
                        P.barrier()
                    with ExitStack() as ph3:
                        W = Wg
                        psets = Ring([[P.ps(ph3, [128, 512], F32, "pst5") for _ in range(4)] for _ in range(2)])
                        xr = Ring([P.sb(ph3, [128, 512], F32, "x5") for _ in range(3)])
                        gcr = Ring([P.sb(ph3, [128, 512], F32, "g2bc") for _ in range(2)])
                        tmr = Ring([P.sb(ph3, [128, 512], F32, "tm5") for _ in range(2)])
                        jkr = Ring([P.sb(ph3, [128, 512], BF16, "jk5") for _ in range(2)])
                        P.op("dve", lambda e: e.memset(ssq.t[:], 0.0), writes=[ssq])
                        gstate = {}

                        def epi_dn(c0, cw, tt, ps):
                            if tt == 0:
                                gc = gcr.next()
                                P.dma("sp", gc.t[:, 0:cw],
                                      modrow[0:1, 5 * D + c0:5 * D + c0 + cw].partition_broadcast(128), writes=[gc])
                                gstate["gc"] = gc
                            gc = gstate["gc"]
                            xt = xr.next()
                            P.dma("sp", xt.t[:, 0:cw], X1[t0 + tt * 128:t0 + (tt + 1) * 128, c0:c0 + cw], writes=[xt])
                            tm = tmr.next()
                            P.op("dve", lambda e: e.tensor_tensor(out=tm.t[:, 0:cw], in0=ps.t[:, 0:cw], in1=gc.t[:, 0:cw],
                                                                  op=ALU.mult), reads=[ps, gc], writes=[tm])
                            P.op("dve", lambda e: e.tensor_tensor(out=xt.t[:, 0:cw], in0=tm.t[:, 0:cw], in1=xt.t[:, 0:cw],
                                                                  op=ALU.add), reads=[tm, xt], writes=[xt])
                            jk = jkr.next()
                            cbi = c0 // 512
                            P.op("act", lambda e: e.activation(out=jk.t[:, 0:cw], in_=xt.t[:, 0:cw], func=AF.Square,
                                                               accum_out=ssq.t[:, tt, cbi:cbi + 1]),
                                 reads=[xt], writes=[jk, ssq])
                            P.dma("sp", out_d[t0 + tt * 128:t0 + (tt + 1) * 128, c0:c0 + cw], xt.t[:, 0:cw], reads=[xt])
                        gemm_tm(W, wd_d, NFF, lambda k, tt: actT.t[:, k, tt * 128:(tt + 1) * 128], D, psets, epi_dn,
                                [actT])
                        P.barrier()
                    with ExitStack() as ph3:
                        ncb = (D + 511) // 512
                        fgb = P.sb(ph3, [128, D], F32, "fgb")
                        P.dma("sp", fgb.t[:], fg_d[0:1, :].partition_broadcast(128), writes=[fgb])
                        rowr = Ring([P.sb(ph3, [128, D], F32, "frow") for _ in range(2)])
                        rs = P.sb(ph3, [128, 8], F32, "rsf")
                        for tt in range(4):
                            row = rowr.next()
                            P.dma("sp", row.t[:], out_d[t0 + tt * 128:t0 + (tt + 1) * 128, :], writes=[row])
                            P.op("dve", lambda e, tt=tt: e.tensor_reduce(out=rs.t[:, tt:tt + 1], in_=ssq.t[:, tt, 0:ncb],
                                                                         axis=mybir.AxisListType.X, op=ALU.add),
                                 reads=[ssq], writes=[rs])
                            rsqrt_ops(rs.t[:, tt:tt + 1], [rs], rs.t[:, tt:tt + 1], rs, 1.0 / D)
                            P.op("dve", lambda e, tt=tt, row=row: e.scalar_tensor_tensor(
                                out=row.t[:], in0=row.t[:], scalar=rs.t[:, tt:tt + 1], in1=fgb.t[:],
                                op0=ALU.mult, op1=ALU.mult), reads=[row, rs, fgb], writes=[row])
                            P.dma("sp", out_d[t0 + tt * 128:t0 + (tt + 1) * 128, :], row.t[:], reads=[row])
                        P.barrier()
                        chk(7)
        except _Stop:
            pass

        with nc.Block() as block:
            @block.tensor
            def _(e):
                for t in P.q["pe"]:
                    t(e)

            @block.scalar
            def _(e):
                for t in P.q["act"]:
                    t(e)

            @block.vector
            def _(e):
                for t in P.q["dve"]:
                    t(e)

            @block.gpsimd
            def _(e):
                for t in P.q["pool"]:
                    t(e)

            @block.sync
            def _(e):
                for t in P.q["sp"]:
                    t(e)
    return nc


def host_constants():
    bf = ml_dtypes.bfloat16
    ident = np.eye(128, dtype=np.float32).astype(bf)
    j = np.arange(128)[:, None]
    i = np.arange(128)[None, :]
    masks = np.zeros((128, NSEQ, 128), np.float32)
    for s, (g, rel) in enumerate(DIL_SEQ):
        delta = rel * 128 + j - i
        masks[:, s, :] = ((delta % DIL_D[g] == 0) & (np.abs(delta) <= DIL_HALF[g])).astype(np.float32)
    masks = masks.reshape(128, NSEQ * 128).astype(bf)
    p = np.arange(128)
    ropec = np.zeros((128, 4), np.float32)
    ropec[:, 0] = THETA ** (-(2.0 * (p % 64)) / 128.0) / (2 * PI)
    ropec[:, 1] = -2 * PI * np.where(p < 64, -1.0, 1.0)
    ropec[:, 2] = THETA ** (-(2.0 * (p % 32)) / 64.0) / (2 * PI)
    ropec[:, 3] = -2 * PI * np.where(p % 64 < 32, -1.0, 1.0)
    return ident, masks, ropec


def col_layout(v, n):
    return np.ascontiguousarray(np.asarray(v, np.float32).reshape(n, 128).T)


_CACHE = {}


def run(inputs, n_batch):
    x = np.asarray(inputs["x"], np.float32)
    B, S, D = x.shape
    assert S == SEQ and B == n_batch
    DFF = inputs["w_gate"].shape[-1]
    KC = D // 128
    key = (D, DFF)
    if key not in _CACHE:
        _CACHE[key] = build_program(D, DFF)
    nc = _CACHE[key]
    ident, masks, ropec = host_constants()
    c = np.asarray(inputs["c"], np.float32)
    pos = np.asarray(inputs["positions"], np.int32)
    w_in = np.ascontiguousarray(np.asarray(inputs["w_in"], np.float32)[0])
    kr = w_in[:, 1536:1600]
    w_krp = np.ascontiguousarray(np.concatenate([kr[:, 32:64], kr[:, 0:32]], axis=1))
    wuq = np.asarray(inputs["w_uq"], np.float32)[0].reshape(QRANK, MLA_H, 192)
    wuq_r = wuq[:, :, 128:192]
    wuq2 = np.ascontiguousarray(np.concatenate(
        [wuq[:, :, 0:128], wuq_r, wuq_r[:, :, 32:64], wuq_r[:, :, 0:32]], axis=2).reshape(QRANK, MLA_H * 256))
    wukv = np.asarray(inputs["w_ukv"], np.float32)[0].reshape(KVRANK, MLA_H, 256)
    wukv2 = np.ascontiguousarray(np.concatenate(
        [wukv[:, :, 0:128].reshape(KVRANK, -1), wukv[:, :, 128:256].reshape(KVRANK, -1)], axis=1))
    shared = {
        "w_ada": np.ascontiguousarray(np.asarray(inputs["w_ada"], np.float32)[0]),
        "b_ada": np.ascontiguousarray(np.asarray(inputs["b_ada"], np.float32)[0][None, :]),
        "n1g": col_layout(inputs["norm1_g"][0], KC),
        "n2g": col_layout(inputs["norm2_g"][0], KC),
        "qg": col_layout(inputs["q_norm_g"][0], QRANK // 128),
        "kvg": col_layout(inputs["kv_norm_g"][0], KVRANK // 128),
        "fg": np.ascontiguousarray(np.asarray(inputs["final_g"], np.float32)[None, :]),
        "w_in": w_in, "w_krp": w_krp, "w_uq": wuq2, "w_ukv": wukv2,
        "w_pa": np.ascontiguousarray(np.asarray(inputs["w_proj_a"], np.float32)[0]),
        "w_pb": np.ascontiguousarray(np.asarray(inputs["w_proj_b"], np.float32)[0]),
        "w_out": np.ascontiguousarray(np.asarray(inputs["w_out"], np.float32)[0]),
        "w_gate": np.ascontiguousarray(np.asarray(inputs["w_gate"], np.float32)[0]),
        "w_up": np.ascontiguousarray(np.asarray(inputs["w_up"], np.float32)[0]),
        "w_down": np.ascontiguousarray(np.asarray(inputs["w_down"], np.float32)[0]),
        "ident": ident, "masks": masks, "ropec": ropec,
    }
    in_maps = []
    perms = []
    for b in range(B):
        for hf in range(2):
            perm = np.arange(SEQ) if hf == 0 else np.arange(SEQ - 1, -1, -1)
            perms.append(perm)
            m = dict(shared)
            m["x"] = np.ascontiguousarray(x[b][perm])
            m["pos"] = np.ascontiguousarray(pos[b][perm][None, :])
            m["c_t"] = col_layout(c[b], KC)
            in_maps.append(m)
    import os
    for alloc in nc.allocations:
        if isinstance(alloc, mybir.MemoryLocationSet) and alloc.kind == "ExternalInput" and alloc.tensor_shape is not None:
            nm = alloc.memorylocations[0].name
            if nm in in_maps[0]:
                a = in_maps[0][nm]
                if tuple(a.shape) != tuple(alloc.tensor_shape) or a.dtype != mybir.dt.np(alloc.dtype):
                    print("MISMATCH", nm, a.shape, a.dtype, alloc.tensor_shape, alloc.dtype)
            else:
                print("MISSING", nm)
    if os.environ.get("K1CORE"):
        in_maps = in_maps[:1]
        res = run_bass_kernel_spmd(nc, in_maps, core_ids=[0])
        o = res.results[0]["out"]
        global LAST
        LAST = res.results[0]
        out = np.zeros((B, S, D), np.float32)
        out[0][perms[0][:NOWN]] = o
        return out
    res = run_bass_kernel_spmd(nc, in_maps, core_ids=list(range(2 * B)))
    out = np.empty((B, S, D), np.float32)
    for b in range(B):
        for hf in range(2):
            o = res.results[b * 2 + hf]["out"]
            out[b][perms[b * 2 + hf][:NOWN]] = o
    return out


def kernel(**inputs):
    return run(inputs, 4)
```
